# Optimizing a Trainium2 kernel written in Bass

```python
import jax, jax.numpy as jnp
from jax import lax
import numpy as np

D_MODEL = 1024
BATCH = 16
SEQ = 256
DEPTH = 2
DEC_BATCH = 4
DEC_SEQ = 1024
PAST_LEN = 512

GRID_W = 64
N_BRANCH = 4
BR_W = D_MODEL // 4
CHUNK = 128
A_GROUPS = 4
A_GD = BR_W // A_GROUPS
RW_HD = 64
RW_HEADS = BR_W // RW_HD
DECAY_RANK = 64
ICL_RANK = 64
ATT_HD = 64
ATT_HEADS = BR_W // ATT_HD
ATT_KV_HEADS = 2
ATT_GROUP = ATT_HEADS // ATT_KV_HEADS
WINDOW = 128
ROPE_BASE = 10000.0
POOL_SIZES = (2, 4, 8, 16)
POOL_GD = BR_W // len(POOL_SIZES)
NORM_EPS = 1e-6
GN_EPS = 64e-5
NEG_INF = -1e30
IN_SIZES = (BR_W, BR_W, BR_W, BR_W, BR_W, DECAY_RANK, ICL_RANK, ATT_HEADS * ATT_HD, ATT_KV_HEADS * ATT_HD, ATT_KV_HEADS * ATT_HD, BR_W, N_BRANCH * BR_W, N_BRANCH * D_MODEL)
IN_W = sum(IN_SIZES)

kernel_name = 'hybrid_flow_gated_branches_step'


def rmsnorm(x, g):
    xf = x.astype(jnp.float32)
    y = xf * lax.rsqrt(jnp.mean(xf * xf, axis=-1, keepdims=True) + NORM_EPS)
    return (y * g.astype(jnp.float32)).astype(x.dtype)


def split_in(p):
    cuts = [int(i) for i in np.cumsum(IN_SIZES)[:-1]]
    return jnp.split(p, cuts, axis=-1)


def chunk_mix(u, v, w_s, b_s):
    B, L, _ = u.shape
    nc = L // CHUNK
    vc = v.reshape(B, nc, CHUNK, A_GROUPS, A_GD)
    sv = jnp.einsum('gij,bcjgd->bcigd', w_s, vc) + b_s.T[None, None, :, :, None]
    return u * sv.reshape(B, L, BR_W)


def wkv_scan(s0, r, w, k, v, kk, a, reverse):
    xs = tuple(t.transpose(1, 0, 2, 3) for t in (r, w, k, v, kk, a))

    def step(S, inp):
        r_t, w_t, k_t, v_t, kk_t, a_t = inp
        sa = jnp.einsum('bhvk,bhk->bhv', S, kk_t)
        S = S * w_t[:, :, None, :] - sa[..., None] * (kk_t * a_t)[:, :, None, :] + v_t[..., None] * k_t[:, :, None, :]
        return S, jnp.einsum('bhvk,bhk->bhv', S, r_t)

    s_fin, o = lax.scan(step, s0.astype(jnp.float32), xs, reverse=reverse)
    return s_fin, o.transpose(1, 0, 2, 3)


def rwkv7_mix(r, k, v, wd, ad, s_init, lp):
    B, L, _ = r.shape
    heads = lambda t: t.astype(jnp.float32).reshape(B, L, RW_HEADS, RW_HD)
    r_h, k_h, v_h = heads(r), heads(k), heads(v)
    kk = heads(k * lp['rw_k_k'])
    kk = kk * lax.rsqrt(jnp.sum(kk * kk, axis=-1, keepdims=True) + 1e-12)
    k_a = lp['rw_k_a'].astype(jnp.float32).reshape(RW_HEADS, RW_HD)
    r_k = lp['rw_r_k'].astype(jnp.float32)
    wd_t = jnp.tanh(wd.astype(jnp.float32))
    adf = ad.astype(jnp.float32)
    o_sum = jnp.zeros_like(v_h)
    bonus = jnp.zeros_like(v_h)
    finals = []
    for d, rev in ((0, False), (1, True)):
        w_log = -jax.nn.softplus(-(lp['rw_w0'][d] + wd_t @ lp['rw_w_up'][d])) - 0.5
        decay = heads(jnp.exp(-jnp.exp(w_log)))
        a = heads(jax.nn.sigmoid(lp['rw_a0'][d] + adf @ lp['rw_a_up'][d]))
        k_d = k_h * (1.0 + (a - 1.0) * k_a)
        s_fin, o = wkv_scan(s_init[:, d], r_h, decay, k_d, v_h, kk, a, rev)
        o_sum = o_sum + o
        bonus = bonus + jnp.sum(r_h * k_d * r_k, axis=-1, keepdims=True) * v_h
        finals.append(s_fin)
    mu = jnp.mean(o_sum, axis=-1, keepdims=True)
    var = jnp.mean(jnp.square(o_sum - mu), axis=-1, keepdims=True)
    on = ((o_sum - mu) * lax.rsqrt(var + GN_EPS)).reshape(B, L, BR_W)
    y = on * lp['rw_ln_g'].astype(jnp.float32) + lp['rw_ln_b'].astype(jnp.float32) + bonus.reshape(B, L, BR_W)
    return y.astype(r.dtype), jnp.stack(finals, axis=1)


def sink_probs(s, sink):
    sk = sink.astype(jnp.float32).reshape(ATT_KV_HEADS, ATT_GROUP, 1)
    m = jnp.maximum(jnp.max(s, axis=-1), sk)
    p = jnp.exp(s - m[..., None])
    den = jnp.sum(p, axis=-1) + jnp.exp(sk - m)
    return p / den[..., None]


def rope_1d(x, pos):
    d = x.shape[-1]
    inv = ROPE_BASE ** (-jnp.arange(0, d, 2, dtype=jnp.float32) / d)
    ang = pos[:, None] * inv[None, :]
    cos, sin = jnp.cos(ang)[None, :, None, :], jnp.sin(ang)[None, :, None, :]
    x1, x2 = x[..., : d // 2], x[..., d // 2:]
    return jnp.concatenate([x1 * cos - x2 * sin, x2 * cos + x1 * sin], axis=-1)


def axial_rope(x):
    L = x.shape[1]
    rows = L // GRID_W
    row = jnp.repeat(jnp.arange(rows, dtype=jnp.float32), GRID_W)
    col = jnp.tile(jnp.arange(GRID_W, dtype=jnp.float32), rows)
    xf = x.astype(jnp.float32)
    half = x.shape[-1] // 2
    return jnp.concatenate([rope_1d(xf[..., :half], row), rope_1d(xf[..., half:], col)], axis=-1)


def ctx_attention(q, k, v, sink):
    B, L, _ = q.shape
    nb = L // CHUNK
    scale = ATT_HD ** -0.5
    qb = q.astype(jnp.float32).reshape(B, nb, CHUNK, ATT_KV_HEADS, ATT_GROUP, ATT_HD).transpose(1, 0, 2, 3, 4, 5)
    kf = k.astype(jnp.float32).reshape(B, L, ATT_KV_HEADS, ATT_HD)
    vf = v.astype(jnp.float32).reshape(B, L, ATT_KV_HEADS, ATT_HD)

    def blk(qi):
        s = jnp.einsum('bqhgd,bkhd->bhgqk', qi, kf) * scale
        return jnp.einsum('bhgqk,bkhd->bqhgd', sink_probs(s, sink), vf)

    o = lax.map(blk, qb)
    return o.transpose(1, 0, 2, 3, 4, 5).reshape(B, L, ATT_HEADS * ATT_HD).astype(q.dtype)


def latent_attention(q, k, v, ck, cv, sink):
    B, L, _ = q.shape
    nb = L // CHUNK
    scale = ATT_HD ** -0.5
    qh = axial_rope(q.reshape(B, L, ATT_HEADS, ATT_HD)).reshape(B, nb, CHUNK, ATT_KV_HEADS, ATT_GROUP, ATT_HD)
    kh = axial_rope(k.reshape(B, L, ATT_KV_HEADS, ATT_HD))
    vh = v.astype(jnp.float32).reshape(B, L, ATT_KV_HEADS, ATT_HD)

    def bands(t):
        tp = jnp.pad(t, ((0, 0), (CHUNK, CHUNK), (0, 0), (0, 0))).reshape(B, nb + 2, CHUNK, ATT_KV_HEADS, ATT_HD)
        return jnp.concatenate([tp[:, :-2], tp[:, 1:-1], tp[:, 2:]], axis=2)

    kb, vb = bands(kh), bands(vh)
    blk = jnp.arange(nb)[:, None] * CHUNK
    qpos = blk + jnp.arange(CHUNK)[None, :]
    kpos = blk - CHUNK + jnp.arange(3 * CHUNK)[None, :]
    valid = (jnp.abs(qpos[:, :, None] - kpos[:, None, :]) <= WINDOW) & (kpos[:, None, :] >= 0) & (kpos[:, None, :] < L)
    ckf, cvf = ck.astype(jnp.float32), cv.astype(jnp.float32)
    s_loc = jnp.einsum('bnqhgd,bnkhd->bnhgqk', qh, kb) * scale
    s_loc = jnp.where(valid[None, :, None, None], s_loc, NEG_INF)
    s_ctx = jnp.einsum('bnqhgd,bkhd->bnhgqk', qh, ckf) * scale
    p = sink_probs(jnp.concatenate([s_loc, s_ctx], axis=-1), sink)
    nl = 3 * CHUNK
    o = jnp.einsum('bnhgqk,bnkhd->bnqhgd', p[..., :nl], vb) + jnp.einsum('bnhgqk,bkhd->bnqhgd', p[..., nl:], cvf)
    return o.reshape(B, L, ATT_HEADS * ATT_HD).astype(q.dtype)


def pool_mix(p, pool_w, pool_scale):
    B, L, _ = p.shape
    pg = p.astype(jnp.float32).reshape(B, L, len(POOL_SIZES), POOL_GD)
    t = jnp.arange(L)
    outs = []
    for g, w in enumerate(POOL_SIZES):
        xg = pg[:, :, g]
        cs = jnp.concatenate([jnp.zeros((B, 1, POOL_GD), jnp.float32), jnp.cumsum(xg, axis=1)], axis=1)
        lo = jnp.clip(t - w // 2, 0, L)
        hi = jnp.clip(t - w // 2 + w, 0, L)
        mean = (cs[:, hi] - cs[:, lo]) / (hi - lo).astype(jnp.float32)[None, :, None]
        outs.append(mean - xg)
    d = jnp.stack(outs, axis=2)
    y = jnp.einsum('blgc,gcd->blgd', d, pool_w.astype(jnp.float32)).reshape(B, L, BR_W) * pool_scale.astype(jnp.float32)
    return y.astype(p.dtype)


def trunk_layer(x, cond, lp, ctx=None):
    B, L, _ = x.shape
    mod = jax.nn.silu(cond) @ lp['w_mod'] + lp['b_mod']
    shift, scale, gate = jnp.split(mod[:, None, :], 3, axis=-1)
    h = rmsnorm(x, lp['g_norm']) * (1 + scale) + shift
    (a_u, a_v, b_r, b_k, b_v, b_wd, b_ad, c_q, c_k, c_v, d_p, z, mg) = split_in(h @ lp['w_in'])
    y_a = chunk_mix(a_u, a_v, lp['w_s'], lp['b_s'])
    if ctx is None:
        s0 = jnp.zeros((B, 2, RW_HEADS, RW_HD, RW_HD), jnp.float32)
        y_c = ctx_attention(c_q, c_k, c_v, lp['att_sink'])
    else:
        ck, cv, s0 = ctx
        y_c = latent_attention(c_q, c_k, c_v, ck, cv, lp['att_sink'])
    y_b, s_fin = rwkv7_mix(b_r, b_k, b_v, b_wd, b_ad, s0, lp)
    y_d = pool_mix(d_p, lp['pool_w'], lp['pool_scale'])
    ys = jnp.stack([y_a, y_b, y_c, y_d], axis=2) * jax.nn.silu(z).reshape(B, L, N_BRANCH, BR_W)
    up = jnp.einsum('blnc,ncd->blnd', ys, lp['w_up'])
    merged = jnp.sum(jax.nn.sigmoid(mg.reshape(B, L, N_BRANCH, D_MODEL)) * up, axis=2)
    x = x + gate * (merged @ lp['w_o'])
    ctx_k = c_k.reshape(B, L, ATT_KV_HEADS, ATT_HD)
    ctx_v = c_v.reshape(B, L, ATT_KV_HEADS, ATT_HD)
    return x, ctx_k, ctx_v, s_fin.astype(x.dtype)


def setup_inputs(seed: int = 0) -> dict:
    key = jax.random.key(seed)
    ks = jax.random.split(key, 32)

    def nrm(k, shape, scale):
        return scale * jax.random.normal(k, shape, jnp.float32)

    D = D_MODEL
    NL = DEPTH
    return {
        'x_prompt': nrm(ks[0], (BATCH, SEQ, D), 1.0),
        'x_sample': nrm(ks[1], (DEC_BATCH, DEC_SEQ, D), 1.0),
        'cache_k': nrm(ks[2], (DEC_BATCH, NL, PAST_LEN, ATT_KV_HEADS, ATT_HD), 1.0),
        'cache_v': nrm(ks[3], (DEC_BATCH, NL, PAST_LEN, ATT_KV_HEADS, ATT_HD), 1.0),
        'state_rwkv': nrm(ks[4], (DEC_BATCH, NL, 2, RW_HEADS, RW_HD, RW_HD), 0.3),
        'c': nrm(ks[5], (DEC_BATCH, D), 1.0),
        'c_ctx': nrm(ks[6], (D,), 1.0),
        'w_mod': nrm(ks[7], (NL, D, 3 * D), 0.5 * D ** -0.5),
        'b_mod': nrm(ks[8], (NL, 3 * D), 0.01),
        'g_norm': 1.0 + nrm(ks[9], (NL, D), 0.05),
        'w_in': nrm(ks[10], (NL, D, IN_W), D ** -0.5),
        'w_s': nrm(ks[11], (NL, A_GROUPS, CHUNK, CHUNK), 0.5 * CHUNK ** -0.5),
        'b_s': 1.0 + nrm(ks[12], (NL, A_GROUPS, CHUNK), 0.02),
        'rw_w0': nrm(ks[13], (NL, 2, BR_W), 1.0),
        'rw_w_up': nrm(ks[14], (NL, 2, DECAY_RANK, BR_W), 0.5 * DECAY_RANK ** -0.5),
        'rw_a0': nrm(ks[15], (NL, 2, BR_W), 0.5),
        'rw_a_up': nrm(ks[16], (NL, 2, ICL_RANK, BR_W), 0.5 * ICL_RANK ** -0.5),
        'rw_k_k': 0.85 + nrm(ks[17], (NL, BR_W), 0.05),
        'rw_k_a': 1.0 + nrm(ks[18], (NL, BR_W), 0.05),
        'rw_r_k': nrm(ks[19], (NL, RW_HEADS, RW_HD), 0.1),
        'rw_ln_g': 1.0 + nrm(ks[20], (NL, BR_W), 0.05),
        'rw_ln_b': nrm(ks[21], (NL, BR_W), 0.01),
        'att_sink': nrm(ks[22], (NL, ATT_HEADS), 0.5),
        'pool_w': nrm(ks[23], (NL, len(POOL_SIZES), POOL_GD, POOL_GD), POOL_GD ** -0.5),
        'pool_scale': 1.0 + nrm(ks[24], (NL, BR_W), 0.05),
        'w_up': nrm(ks[25], (NL, N_BRANCH, BR_W, D), BR_W ** -0.5),
        'w_o': nrm(ks[26], (NL, D, D), D ** -0.5),
        'g_final': 1.0 + nrm(ks[27], (D,), 0.05),
    }


def reference(x_prompt, x_sample, cache_k, cache_v, state_rwkv, c, c_ctx, w_mod, b_mod, g_norm, w_in, w_s, b_s, rw_w0, rw_w_up, rw_a0, rw_a_up, rw_k_k, rw_k_a, rw_r_k, rw_ln_g, rw_ln_b, att_sink, pool_w, pool_scale, w_up, w_o, g_final):
    xp, xs = x_prompt, x_sample
    cond_ctx = jnp.broadcast_to(c_ctx[None, :], (xp.shape[0], D_MODEL))
    ks, vs, ss = [], [], []
    for l in range(DEPTH):
        lp = {
            'w_mod': w_mod[l], 'b_mod': b_mod[l], 'g_norm': g_norm[l], 'w_in': w_in[l],
            'w_s': w_s[l], 'b_s': b_s[l],
            'rw_w0': rw_w0[l], 'rw_w_up': rw_w_up[l], 'rw_a0': rw_a0[l], 'rw_a_up': rw_a_up[l],
            'rw_k_k': rw_k_k[l], 'rw_k_a': rw_k_a[l], 'rw_r_k': rw_r_k[l], 'rw_ln_g': rw_ln_g[l], 'rw_ln_b': rw_ln_b[l],
            'att_sink': att_sink[l], 'pool_w': pool_w[l], 'pool_scale': pool_scale[l],
            'w_up': w_up[l], 'w_o': w_o[l],
        }
        xp, k_l, v_l, s_l = trunk_layer(xp, cond_ctx, lp)
        ks.append(k_l)
        vs.append(v_l)
        ss.append(s_l)
        xs, _, _, _ = trunk_layer(xs, c, lp, (cache_k[:, l], cache_v[:, l], state_rwkv[:, l]))
    y_prompt = rmsnorm(xp, g_final)
    y_sample = rmsnorm(xs, g_final)
    new_k = jnp.stack(ks, axis=1)
    new_v = jnp.stack(vs, axis=1)
    new_state = jnp.stack(ss, axis=1)
    return (y_prompt, y_sample, new_k, new_v, new_state)
```

```python
import numpy as np
import concourse.bass as bass
import concourse.mybir as mybir

F32 = mybir.dt.float32
BF16 = mybir.dt.bfloat16
AF = mybir.ActivationFunctionType
ALU = mybir.AluOpType
AX = mybir.AxisListType

ENGS = ("pe", "dve", "act", "pool", "sp")
NDMA_SEMS = 32
SAME_ENGINE_WAR_SYNC = True


class IntervalMap:
    def __init__(self):
        self.segs = []

    def _split(self, lo, hi):
        out = []
        new = []
        cur = lo
        for s in self.segs:
            slo, shi, w, r = s
            if shi <= lo or slo >= hi:
                new.append(s)
                continue
            if slo < lo:
                new.append([slo, lo, w, dict(r)])
                slo = lo
            if shi > hi:
                tail = [hi, shi, w, dict(r)]
                shi = hi
            else:
                tail = None
            if cur < slo:
                g = [cur, slo, None, {}]
                new.append(g)
                out.append(g)
            m = [slo, shi, w, dict(r)]
            new.append(m)
            out.append(m)
            cur = shi
            if tail is not None:
                new.append(tail)
        if cur < hi:
            g = [cur, hi, None, {}]
            new.append(g)
            out.append(g)
        new.sort(key=lambda s: s[0])
        self.segs = new
        return out

    def access(self, lo, hi, is_write, eng, token):
        deps = []
        segs = self._split(lo, hi)
        for s in segs:
            if s[2] is not None:
                deps.append(("w", s[2]))
            if is_write:
                for e, t in s[3].items():
                    deps.append(("r", t))
        for s in segs:
            if is_write:
                s[2] = token
                s[3] = {}
            else:
                s[3][eng if token[0] == "eng" else token] = token
        merged = []
        for s in self.segs:
            if merged and merged[-1][1] == s[0] and merged[-1][2] == s[2] and merged[-1][3] == s[3]:
                merged[-1][1] = s[1]
            else:
                merged.append(s)
        self.segs = merged
        return deps


class Tile:
    def __init__(self, S, off, words, shape, dtype, name):
        self.S = S
        self.off = off
        self.words = words
        self.shape = list(shape)
        self.dtype = dtype
        self.name = name
        P = shape[0]
        free = int(np.prod(shape[1:]))
        base = S.arena[0:P, off:off + words]
        if dtype == BF16:
            base = base.bitcast(BF16)
        base = base[:, 0:free]
        if len(shape) > 2:
            names = " ".join("d%d" % i for i in range(len(shape) - 1))
            kw = {"d%d" % i: shape[i + 1] for i in range(len(shape) - 1)}
            base = base.rearrange("p (%s) -> p %s" % (names, names), **kw)
        self.ap = base
        self.res = ("sb", off, off + words)

    def sub(self, lo_words, hi_words):
        return ("sb", self.off + lo_words, self.off + hi_words)

    def __getitem__(self, k):
        return self.ap[k]


class Banks:
    def __init__(self, n=8):
        self.free = [True] * n
        self.stamp = [0] * n
        self.clock = 0

    def acquire(self, n=1, contiguous=False):
        idx = sorted([i for i, f in enumerate(self.free) if f], key=lambda i: self.stamp[i])
        if contiguous and n == 2:
            best = None
            for i in idx:
                if i + 1 < len(self.free) and self.free[i + 1]:
                    c = max(self.stamp[i], self.stamp[i + 1])
                    if best is None or c < best[0]:
                        best = (c, i)
            if best is None:
                return None
            i = best[1]
            self.free[i] = self.free[i + 1] = False
            return [i, i + 1]
        if len(idx) < n:
            return None
        r = idx[:n]
        for i in r:
            self.free[i] = False
        return r

    def release(self, banks):
        for b in banks:
            assert not self.free[b]
            self.free[b] = True
            self.clock += 1
            self.stamp[b] = self.clock


class Sched:
    def __init__(self, nc, arena_words):
        self.nc = nc
        self.arena_words = arena_words
        self._ctx = []
        g = nc.sbuf_tensor("arena", [128, arena_words], F32)
        self.arena = g.__enter__()
        self._ctx.append(g)
        g = nc.psum_tensor("psum", [128, 4096], F32)
        self.psum = g.__enter__()
        self._ctx.append(g)
        self.free = [(0, arena_words)]
        self.maps = {"sb": IntervalMap(), "ps": IntervalMap()}
        self.prog = {e: [] for e in ENGS}
        self.sems = {}
        for e in ENGS:
            g = nc.semaphore("sem_" + e)
            self.sems[e] = g.__enter__()
            self._ctx.append(g)
        self.dma_sems = []
        for i in range(NDMA_SEMS):
            g = nc.semaphore("dsem%d" % i)
            self.dma_sems.append(g.__enter__())
            self._ctx.append(g)
        self.dma_use = [0] * NDMA_SEMS
        self.dma_nexts = [0, 0]
        self.phase = ''
        self.marks = []
        self.waited = {e: {} for e in ENGS}
        self.peak = 0
        self.nops = 0

    def alloc(self, shape, dtype=F32, name="t"):
        P = shape[0]
        free = int(np.prod(shape[1:]))
        words = free if dtype == F32 else (free + 1) // 2
        words = (words + 7) // 8 * 8
        for i, (lo, hi) in enumerate(self.free):
            if hi - lo >= words:
                self.free[i] = (lo + words, hi)
                if self.free[i][0] == self.free[i][1]:
                    self.free.pop(i)
                used = self.arena_words - sum(h - l for l, h in self.free)
                self.peak = max(self.peak, used)
                return Tile(self, lo, words, shape, dtype, name)
        raise MemoryError("arena full allocating %s %s; free=%s" % (name, shape, self.free))

    def release(self, *tiles):
        for t in tiles:
            self.free.append((t.off, t.off + t.words))
        self.free.sort()
        m = []
        for lo, hi in self.free:
            if m and m[-1][1] == lo:
                m[-1] = (m[-1][0], hi)
            else:
                m.append((lo, hi))
        self.free = m

    def bank(self, b, dtype=F32):
        ap = self.psum[:, b * 512:(b + 1) * 512]
        if dtype == BF16:
            ap = ap.bitcast(BF16)
        return ap

    def banks(self, b, n):
        return self.psum[:, b * 512:(b + n) * 512]

    def _res(self, r):
        if isinstance(r, Tile):
            return r.res
        return r

    def op(self, eng, fn, reads=(), writes=(), dma=False, after=()):
        idx = len(self.prog[eng])
        if dma:
            half = NDMA_SEMS // 2
            grp = 1 if eng == "pool" else 0
            si = grp * half + self.dma_nexts[grp]
            self.dma_nexts[grp] = (self.dma_nexts[grp] + 1) % half
            self.dma_use[si] += 1
            val = 16 * self.dma_use[si]
            token = ("dma", si, val)
        else:
            token = ("eng", eng, idx)
        deps = []
        for r in reads:
            kind, lo, hi = self._res(r)
            if kind == "ps":
                deps += self.maps["ps"].access(lo, hi, True, eng, token)
            else:
                deps += self.maps[kind].access(lo, hi, False, eng, token)
        for r in writes:
            kind, lo, hi = self._res(r)
            deps += self.maps[kind].access(lo, hi, True, eng, token)
        waits = []
        w = self.waited[eng]
        if dma and val > 16:
            deps.append(("w", ("dma", si, val - 16)))
        for t_ in after:
            deps.append(("w", t_))
        for how, t in deps:
            if t == token:
                continue
            if t[0] == "eng":
                _, e2, j = t
                if e2 == eng and not dma:
                    if eng == "pe":
                        continue
                    if how != "w" and not SAME_ENGINE_WAR_SYNC:
                        continue
                key = ("eng", e2)
                if w.get(key, -1) >= j:
                    continue
                w[key] = j
                waits.append(t)
            else:
                _, s2, v = t
                key = ("dma", s2)
                if w.get(key, -1) >= v:
                    continue
                w[key] = v
                waits.append(t)
        best = {}
        for t in waits:
            k = (t[0], t[1])
            if k not in best or best[k][2] < t[2]:
                best[k] = t
        for t in best.values():
            if t[0] == "eng":
                self.prog[t[1]][t[2]]["signal"] = True
        self.prog[eng].append({"fn": fn, "waits": list(best.values()), "signal": False,
                               "dma": (si if dma else None), "phase": self.phase})
        self.nops += 1
        return token

    def dma(self, out_ap, in_ap, reads=(), writes=(), eng="sp", after=(), **kw):
        return self.op(eng, lambda e: e.dma_start(out=out_ap, in_=in_ap, **kw), reads, writes, dma=True, after=after)

    def finish_wait(self, tokens, eng="sp"):
        self.final_waits = (eng, tokens)


    def mm(self, out, lhsT, rhs, start=True, stop=True, R=(), W=()):
        return self.op("pe", lambda e: e.matmul(out, lhsT=lhsT, rhs=rhs, start=start, stop=stop), R, W)

    def tr(self, out, in_, ident, R=(), W=()):
        return self.op("pe", lambda e: e.transpose(out, in_, ident), R, W)

    def act(self, out, in_, func, R=(), W=(), bias=None, scale=None, accum=None):
        kw = {}
        if bias is not None:
            kw["bias"] = bias
        if scale is not None:
            kw["scale"] = scale
        if accum is not None:
            kw["accum_out"] = accum
        return self.op("act", lambda e: e.activation(out=out, in_=in_, func=func, **kw), R, W)

    def tt(self, eng, out, in0, in1, op, R=(), W=()):
        return self.op(eng, lambda e: e.tensor_tensor(out=out, in0=in0, in1=in1, op=op), R, W)

    def ts(self, eng, out, in0, s1, s2, op0, op1=None, R=(), W=()):
        if op1 is None:
            return self.op(eng, lambda e: e.tensor_scalar(out=out, in0=in0, scalar1=s1, scalar2=None, op0=op0), R, W)
        return self.op(eng, lambda e: e.tensor_scalar(out=out, in0=in0, scalar1=s1, scalar2=s2, op0=op0, op1=op1), R, W)

    def stt(self, eng, out, in0, scalar, in1, op0, op1, R=(), W=()):
        return self.op(eng, lambda e: e.scalar_tensor_tensor(out=out, in0=in0, scalar=scalar, in1=in1, op0=op0, op1=op1), R, W)

    def cp(self, eng, out, in_, R=(), W=()):
        if eng == "act":
            return self.op("act", lambda e: e.activation(out=out, in_=in_, func=AF.Copy), R, W)
        return self.op(eng, lambda e: e.tensor_copy(out=out, in_=in_), R, W)

    def red(self, eng, out, in_, op, R=(), W=()):
        return self.op(eng, lambda e: e.tensor_reduce(out=out, in_=in_, axis=AX.X, op=op), R, W)

    def memset(self, eng, ap, val, W=()):
        return self.op(eng, lambda e: e.memset(ap, val), (), W)

    def evac(self, out, in_, R=(), W=()):
        self._ev = getattr(self, "_ev", 0) + 1
        if getattr(self, "evac_all_act", False):
            return self.cp("act", out, in_, R, W)
        return self.cp("act" if self._ev % 2 else "dve", out, in_, R, W)

    def nbank(self, n=1):
        b = getattr(self, "_nb", 0)
        if b + n > 8:
            b = 0
        self._nb = (b + n) % 8
        return b

    def emit(self):
        nc = self.nc
        signum = {}
        for e in ENGS:
            c = 0
            for i, o in enumerate(self.prog[e]):
                if o["signal"] and o["dma"] is None:
                    c += 1
                    signum[(e, i)] = c
        print('[sched] signals:', {e: sum(1 for o in self.prog[e] if o['signal'] and o['dma'] is None) for e in ENGS}, 'ops:', {e: len(self.prog[e]) for e in ENGS})
        handles = {"pe": "tensor", "dve": "vector", "act": "scalar", "pool": "gpsimd", "sp": "sync"}
        final = getattr(self, "final_waits", None)

        def run(ename):
            def body(eng):
                for o in self.prog[ename]:
                    for t in o["waits"]:
                        if t[0] == "eng":
                            eng.wait_ge(self.sems[t[1]], signum[(t[1], t[2])])
                        else:
                            eng.wait_ge(self.dma_sems[t[1]], t[2])
                    inst = o["fn"](eng)
                    if o["phase"] != getattr(self, "_lastph_" + ename, None):
                        setattr(self, "_lastph_" + ename, o["phase"])
                        try:
                            self.marks.append((ename, o["phase"], inst.ins.name))
                        except Exception:
                            pass
                    if o["dma"] is not None:
                        inst.then_inc(self.dma_sems[o["dma"]], 16)
                    elif o["signal"]:
                        inst.then_inc(self.sems[ename], 1)
                if final is not None and final[0] == ename:
                    seen = {}
                    for t in final[1]:
                        seen[t[1]] = max(seen.get(t[1], 0), t[2])
                    for s, v in seen.items():
                        eng.wait_ge(self.dma_sems[s], v)
            return body

        with nc.Block() as block:
            block.tensor(run("pe"))
            block.vector(run("dve"))
            block.scalar(run("act"))
            block.gpsimd(run("pool"))
            block.sync(run("sp"))
        for g in reversed(self._ctx):
            g.__exit__(None, None, None)

from concourse.bass_utils import run_bass_kernel_spmd

D = 1024
NTOK = 1024
NB = 8
KC = 8
DEPTH = 2
IN_W = 7296
C_AU, C_AV, C_R, C_K, C_V, C_WD, C_CQ, C_CK, C_DP, C_Z, C_MG = 0, 256, 512, 768, 1024, 1280, 1408, 1664, 1920, 2176, 3200
NORM_EPS = 1e-6
GN_EPS = 64e-5
DECAY_C = float(np.exp(-0.5))
ARENA_WORDS = 52992
NEG = -1.0e30

FLAGS = {"mix_a": True, "mix_b": True, "mix_c": True, "mix_d": True, "ft32": False, "inv32": False, "attw": 3, "seqmix": False, "step32": False, "rww": 3}


def _pool_full(L):
    mats = []
    t = np.arange(L)
    for w in (2, 4, 8, 16):
        lo = np.clip(t - w // 2, 0, L)
        hi = np.clip(t - w // 2 + w, 0, L)
        M = np.zeros((L, L), np.float64)
        for i in range(L):
            M[i, lo[i]:hi[i]] = 1.0 / (hi[i] - lo[i])
        M -= np.eye(L)
        mats.append(M.T.astype(np.float32))
    return mats


def host_consts(is_sample):
    c = {}
    full = _pool_full(1024 if is_sample else 256)
    nblk = len(full[0]) // 128
    pm = np.zeros((128, 8, 4, 128), np.float32)

    def blk(g, sb, tb):
        return full[g][sb * 128:(sb + 1) * 128, tb * 128:(tb + 1) * 128]
    for g in range(4):
        S_ = blk(g, 0, 0)
        E_ = blk(g, nblk - 1, nblk - 1)
        if is_sample:
            I_ = blk(g, 1, 1)
            L_ = blk(g, 0, 1)
            R_ = blk(g, 2, 1)
            slots = [S_, E_, I_, I_, L_, L_, R_, R_]
        else:
            L_ = blk(g, 0, 1)
            R_ = blk(g, 1, 0)
            Z_ = np.zeros_like(L_)
            slots = [S_, E_, E_, S_, L_, Z_, R_, Z_]
        for si, m in enumerate(slots):
            pm[:, si, g, :] = m
    c["poolm"] = pm.reshape(128, 8 * 4 * 128)
    i = np.arange(128)[:, None]
    j = np.arange(128)[None, :]
    allv = np.zeros((128, 128), np.float32)
    inv = np.full((128, 128), NEG, np.float32)
    tril = np.where(j >= i, 0.0, NEG).astype(np.float32)
    trir = np.where(j <= i, 0.0, NEG).astype(np.float32)
    am = np.zeros((128, 4, 896), np.float32)
    if is_sample:
        loc = [[inv, allv, trir], [tril, allv, inv], [tril, allv, trir], [tril, allv, trir]]
        ctxv = 0.0
    else:
        loc = [[inv, allv, allv], [allv, allv, inv], [allv, allv, inv], [inv, allv, allv]]
        ctxv = NEG
    for s in range(4):
        am[:, s, 0:512] = ctxv
        am[:, s, 512:896] = np.concatenate(loc[s], axis=1)
    c["amask"] = am.reshape(128, 4 * 896)
    cos = np.ones((64, 1024), np.float32)
    sin = np.zeros((64, 1024), np.float32)
    if is_sample:
        pos = np.arange(1024)
        row = (pos // 64).astype(np.float32)
        col = (pos % 64).astype(np.float32)
        inv_f = (10000.0 ** (-np.arange(0, 32, 2, dtype=np.float32) / 32)).astype(np.float32)
        for hd in range(64):
            p = row if hd < 32 else col
            jj = hd % 32
            f = inv_f[jj % 16]
            ang = (p * f).astype(np.float32)
            cos[hd] = np.cos(ang)
            sin[hd] = (-np.sin(ang)) if jj < 16 else np.sin(ang)
    c["ropec"] = np.concatenate([cos, cos], 0)
    c["ropes"] = np.concatenate([sin, sin], 0)
    return c


def host_shared_consts():
    c = {}
    perm = np.zeros((128, 128), np.float32)
    for ii in range(128):
        jj = ii % 32
        partner = ii + 16 if jj < 16 else ii - 16
        perm[partner, ii] = 1.0
    c["perm"] = perm
    c["ident"] = np.eye(128, dtype=np.float32)
    s = np.arange(128)[:, None]
    t = np.arange(128)[None, :]
    tri = np.zeros((128, 2, 4, 128), np.float32)
    msk = np.zeros((128, 2, 4, 128), np.float32)
    mskT = np.zeros((128, 2, 128), np.float32)
    for d in range(2):
        incl = (s <= t) if d == 0 else (s >= t)
        strict = (s < t) if d == 0 else (s > t)
        tri[:, d, 0, :] = -DECAY_C * incl
        tri[:, d, 1, :] = -DECAY_C * strict
        tri[:, d, 2, :] = -DECAY_C * (1.0 - incl)
        msk[:, d, 0, :] = strict
        msk[:, d, 1, :] = incl
        msk[:, d, 2, :] = strict
        msk[:, d, 3, :] = incl
        mskT[:, d, :] = strict.T
    c["rwtri"] = tri.reshape(128, 2 * 4 * 128)
    c["rwmsk"] = msk.reshape(128, 2 * 4 * 128)
    c["rwmskT"] = mskT.reshape(128, 2 * 128)
    return c


def build_program():
    nc = bass.Bass("TRN2", target_bir_lowering=False)

    def din(name, shape):
        return nc.dram_tensor(name, list(shape), F32, kind="ExternalInput").ap()

    def dout(name, shape):
        return nc.dram_tensor(name, list(shape), F32, kind="ExternalOutput").ap()

    xin = din("xin", [NTOK, D])
    condT = din("condT", [128, KC])
    w_mod = din("w_mod", [DEPTH, D, 3 * D])
    b_modT = din("b_modT", [DEPTH, 128, 24])
    g_normT = din("g_normT", [DEPTH, 128, KC])
    g_fin_b = din("g_fin_b", [128, D])
    w_in = din("w_in", [DEPTH, D, IN_W])
    w_sT = din("w_sT", [DEPTH, 128, 4 * 128])
    b_sT = din("b_sT", [DEPTH, 128, 2 * 128])
    poolw = din("poolw", [DEPTH, 128, 2 * 256])
    pscaleT = din("pscaleT", [DEPTH, 128, 2])
    sinkb = din("sinkb", [DEPTH, 128, 8])
    w_up = din("w_up", [DEPTH, 4, 256, D])
    w_o = din("w_o", [DEPTH, D, D])
    ckT = din("ckT", [DEPTH, 2, 64, 512])
    cvd = din("cvd", [DEPTH, 512, 256])
    poolm = din("poolm", [128, 8 * 4 * 128])
    amask = din("amask", [128, 4 * 896])
    ropec = din("ropec", [128, NTOK])
    ropes = din("ropes", [128, NTOK])
    perm = din("perm", [128, 128])
    ident = din("ident", [128, 128])
    rwtri = din("rwtri", [128, 1024])
    rwmsk = din("rwmsk", [128, 1024])
    rwmskT = din("rwmskT", [128, 256])
    rwvec = din("rwvec", [DEPTH, 128, 5 * 256])
    rwrow = din("rwrow", [DEPTH, 4 * 256])
    rwup = din("rwup", [DEPTH, 128, 4 * 256])
    sinit = din("sinit", [DEPTH, 2, 4, 64, 256])
    keepc = din("keepc", [64, 1])

    y_out = dout("y_out", [NTOK, D])
    k_out = dout("k_out", [DEPTH, NTOK, 128])
    v_out = dout("v_out", [DEPTH, NTOK, 128])
    s_out = dout("s_out", [DEPTH, 2, 4, 64, 256])

    S = Sched(nc, ARENA_WORDS)
    S.extra_out = []
    out_tokens = []
    PS = lambda b, n=1: ("ps", b, b + n)

    Xh = [S.alloc([128, NB, D], F32, "X")]
    xs = nc.dram_tensor("xs_scratch", [NTOK, D], F32).ap()
    xs_tok = []
    identf = S.alloc([128, 128], F32, "identf")
    identb = S.alloc([128, 128], BF16, "identb")
    onesf = S.alloc([128, 128], F32, "onesf")
    modT = [S.alloc([128, 24], F32, "modT%d" % l) for l in range(DEPTH)]
    gnT = S.alloc([128, DEPTH, KC], F32, "gnT")
    keep = S.alloc([64, 1], F32, "keep")

    S.dma(Xh[0].ap, xin.rearrange("(n p) d -> p n d", p=128), writes=[Xh[0]])
    S.dma(identf.ap, ident, writes=[identf])
    S.dma(identb.ap, ident, writes=[identb], eng="pool")
    S.dma(gnT.ap, g_normT.rearrange("l p k -> p l k"), writes=[gnT])
    S.dma(keep.ap, keepc, writes=[keep])
    S.memset("dve", onesf.ap, 1.0, W=[onesf])
    mhalf = S.alloc([128, 8], F32, "mhalf")
    S.memset("dve", mhalf.ap, -0.5, W=[mhalf])

    S.phase = 'mod'
    sc = S.alloc([128, KC], F32, "sc")
    scb = S.alloc([128, KC], BF16, "scb")
    S.dma(sc.ap, condT, writes=[sc])
    S.act(scb.ap, sc.ap, AF.Silu, R=[sc], W=[scb])
    bmt = S.alloc([128, DEPTH, 24], F32, "bmt")
    S.dma(bmt.ap, b_modT.rearrange("l p c -> p l c"), writes=[bmt])
    S.release(sc)

    def mod_gen(l):
        wmb = [S.alloc([128, KC, 512], BF16, "wm%d" % i) for i in range(2)]
        mrow = S.alloc([1, 3 * D], F32, "mrow")
        wsrc = w_mod[l].rearrange("(kc p) c -> p kc c", p=128)
        S.dma(wmb[0].ap, wsrc[:, :, 0:512], writes=[wmb[0]], eng="pool")
        for ci in range(6):
            if ci + 1 < 6:
                S.dma(wmb[(ci + 1) % 2].ap, wsrc[:, :, (ci + 1) * 512:(ci + 2) * 512], writes=[wmb[(ci + 1) % 2]], eng="pool")
            yield
            wb = wmb[ci % 2]
            bk = S.nbank()
            for kc in range(KC):
                S.mm(S.bank(bk)[0:1, :], scb.ap[:, kc:kc + 1], wb.ap[:, kc, :], start=(kc == 0), stop=(kc == KC - 1),
                     R=[wb, scb], W=[PS(bk)])
            S.cp("act", mrow.ap[0:1, ci * 512:(ci + 1) * 512], S.bank(bk)[0:1, :], R=[PS(bk)], W=[mrow])
        yield
        bk = S.nbank()
        for j in range(24):
            S.mm(S.bank(bk)[:, j:j + 1], mrow.ap[0:1, j * 128:(j + 1) * 128], onesf.ap[0:1, 0:1], R=[mrow, onesf], W=[PS(bk)])
        S.tt("dve", modT[l].ap, S.bank(bk)[:, 0:24], bmt.ap[:, l, :], ALU.add, R=[PS(bk), bmt], W=[modT[l]])
        S.release(mrow, *wmb)

    for _ in mod_gen(0):
        pass
    mod1 = [mod_gen(1)]


    def recip(out, in_, R, W):
        S.op("dve", (lambda o, i: (lambda e: e.reciprocal(out=o, in_=i)))(out, in_), reads=R, writes=W)

    def rsqrt_inplace(t):
        P_, n_ = t.shape[0], int(np.prod(t.shape[1:]))
        S.tt("pool", t.ap, t.ap, mhalf.ap[0:P_, 0:n_], ALU.pow, R=[t, mhalf], W=[t])

    def bcast_tile(dst, colsrc, R):
        b0 = S.nbank(2)
        dg = S.alloc([128, 128], F32, "dg")
        for c in range(KC):
            S.ts("dve", dg.ap, identf.ap, colsrc[:, c:c + 1], None, ALU.mult, R=[identf] + R, W=[dg])
            S.mm(S.banks(b0, 2)[:, c * 128:(c + 1) * 128], onesf.ap, dg.ap, R=[onesf, dg], W=[PS(b0 + c // 4)])
        S.cp("act", dst.ap[:, 0:512], S.bank(b0), R=[PS(b0)], W=[dst.sub(0, 512)])
        S.cp("dve", dst.ap[:, 512:1024], S.bank(b0 + 1), R=[PS(b0 + 1)], W=[dst.sub(512, 1024)])
        S.release(dg)

    def rms_stats(xb, rstd, R):
        junk = S.alloc([128, D], BF16, "junk")
        ss = S.alloc([128, 1], F32, "ss")
        S.act(junk.ap, xb, AF.Square, R=R, W=[junk, ss], accum=ss.ap)
        S.ts("dve", rstd.ap, ss.ap, 1.0 / D, NORM_EPS, ALU.mult, ALU.add, R=[ss], W=[rstd])
        rsqrt_inplace(rstd)
        S.release(junk, ss)

    NRING = 4
    ring = []
    ring_i = [0]

    def ring_alloc():
        ring.extend(S.alloc([128, KC, 256], BF16, "ring%d" % i) for i in range(NRING))

    def ring_free():
        S.release(*ring)
        del ring[:]

    def load_w_in(l, c0, ncols):
        t = ring[ring_i[0] % NRING]
        ring_i[0] += 1
        S.dma(t.ap[:, :, 0:ncols], w_in[l].rearrange("(kc p) c -> p kc c", p=128)[:, :, c0:c0 + ncols],
              writes=[t], eng="pool")
        return t


    BA = Banks(8)
    STARVE = [0, 0]

    def acq(n=1, contiguous=False):
        while True:
            r_ = BA.acquire(n, contiguous)
            if r_ is not None:
                return r_
            STARVE[0] += 1
            yield

    def acq_level(need3):
        while True:
            r_ = BA.acquire(2, True)
            if r_ is not None:
                if not need3:
                    return r_
                r2 = BA.acquire(1)
                if r2 is not None:
                    return r_ + r2
                BA.release(r_)
            STARVE[1] += 1
            yield

    class BankPool:
        def __init__(self, lo, hi):
            self.lo, self.hi, self.cur = lo, hi, lo

        def get(self, n=1):
            if self.cur + n > self.hi:
                self.cur = self.lo
            b = self.cur
            self.cur += n
            if self.cur >= self.hi:
                self.cur = self.lo
            return b

    def rr(gens):
        act = list(gens)
        while act:
            for g in list(act):
                try:
                    next(g)
                except StopIteration:
                    act.remove(g)

    def interleave(gens, width):
        gens = list(gens)
        active = []
        while gens or active:
            while gens and len(active) < width:
                active.append(gens.pop(0))
            for g in list(active):
                try:
                    next(g)
                except StopIteration:
                    active.remove(g)
                yield

    def rwkv_gen(l, zs, rkv, wdadT):
        BP = BankPool(0, 4)
        FTD = F32 if FLAGS.get('ft32', True) else BF16
        INVD = F32 if FLAGS.get('inv32', True) else BF16
        STD = F32 if FLAGS.get('step32', False) else BF16
        idF = identf if FTD == F32 else identb
        idI = identf if INVD == F32 else identb
        vec = S.alloc([128, 5, 256], F32, "rwvec")
        rows = S.alloc([1, 4, 256], F32, "rwrows")
        ups = S.alloc([128, 4, 256], BF16, "rwups")
        rows_hi = S.alloc([1, 4, 256], BF16, "rwrows_hi")
        rows_lo = S.alloc([1, 4, 256], BF16, "rwrows_lo")
        onesb1 = S.alloc([1, 128], BF16, "onesb1")
        tri = S.alloc([128, 2, 4, 128], F32, "rwtri")
        msk = S.alloc([128, 2, 4, 128], BF16, "rwmsk")
        mskT = S.alloc([128, 2, 128], BF16, "rwmskT")
        negc = S.alloc([128, 1], F32, "negc")
        vb = S.alloc([128, NB, 256], BF16, "vb")
        S.dma(vec.ap, rwvec[l].rearrange("p (a c) -> p a c", a=5), writes=[vec])
        S.dma(rows.ap, rwrow[l:l + 1, :].rearrange("o (a c) -> o a c", a=4), writes=[rows])
        S.dma(ups.ap, rwup[l].rearrange("p (a c) -> p a c", a=4), writes=[ups], eng="pool")
        S.memset("dve", onesb1.ap, 1.0, W=[onesb1])
        S.cp("dve", rows_hi.ap, rows.ap, R=[rows], W=[rows_hi])
        S.tt("dve", rows.ap, rows.ap, rows_hi.ap, ALU.subtract, R=[rows, rows_hi], W=[rows])
        S.cp("dve", rows_lo.ap, rows.ap, R=[rows], W=[rows_lo])
        S.dma(tri.ap, rwtri.rearrange("p (d a t) -> p d a t", d=2, a=4), writes=[tri])
        S.dma(msk.ap, rwmsk.rearrange("p (d a t) -> p d a t", d=2, a=4), writes=[msk], eng="pool")
        S.dma(mskT.ap, rwmskT.rearrange("p (d t) -> p d t", d=2), writes=[mskT], eng="pool")
        S.memset("dve", negc.ap, -DECAY_C, W=[negc])
        S.cp("dve", vb.ap, rkv.ap[:, :, 512:768], R=[rkv], W=[vb])
        ones1 = onesf.ap[0:1, :]
        osum = S.alloc([128, NB, 256], F32, "osum")
        bsum = S.alloc([128, NB, 4], F32, "bsum")
        ST = [[S.alloc([64, 4, 64], F32, "ST%d%d" % (d, i)) for i in range(2)] for d in range(2)]
        SB = [S.alloc([64, 4, 64], BF16, "SB%d" % d) for d in range(2)]
        touched_o = set()
        touched_b = set()
        yield

        def prep(b, d, P_):
            r = rkv.ap[:, b, 0:256]
            k = rkv.ap[:, b, 256:512]
            kk = S.alloc([128, 256], F32, "kk")
            t0 = S.alloc([128, 256], F32, "t0")
            s4 = S.alloc([128, 4], F32, "s4")
            S.tt("dve", kk.ap, k, vec.ap[:, 0, :], ALU.mult, R=[rkv, vec], W=[kk])
            S.tt("dve", t0.ap, kk.ap, kk.ap, ALU.mult, R=[kk], W=[t0])
            S.red("dve", s4.ap, t0.ap.rearrange("p (h c) -> p h c", h=4), ALU.add, R=[t0], W=[s4])
            S.ts("dve", s4.ap, s4.ap, 1e-12, None, ALU.add, R=[s4], W=[s4])
            bk, = yield from acq(1)
            wsl = wdadT.ap[:, b * 128:(b + 1) * 128]
            for q in range(2):
                S.mm(S.bank(bk)[:, q * 256:(q + 1) * 256], wsl, ups.ap[:, q * 2 + d, :], start=True, stop=False,
                     R=[wdadT, ups], W=[PS(bk)])
                S.mm(S.bank(bk)[:, q * 256:(q + 1) * 256], onesb1.ap, rows_hi.ap[0:1, q * 2 + d, :], start=False, stop=False,
                     R=[onesb1, rows_hi], W=[PS(bk)])
                S.mm(S.bank(bk)[:, q * 256:(q + 1) * 256], onesb1.ap, rows_lo.ap[0:1, q * 2 + d, :], start=False, stop=True,
                     R=[onesb1, rows_lo], W=[PS(bk)])
            yield
            rsqrt_inplace(s4)
            sga = S.alloc([128, 512], F32, "sga")
            S.act(sga.ap, S.bank(bk), AF.Tanh, R=[PS(bk)], W=[sga], scale=0.5)
            BA.release([bk])
            S.ts("dve", sga.ap, sga.ap, 0.5, 0.5, ALU.mult, ALU.add, R=[sga], W=[sga])
            sg = sga.ap[:, 0:256]
            a = sga.ap[:, 256:512]
            yield
            S.tt("dve", kk.ap.rearrange("p (h c) -> p h c", h=4), kk.ap.rearrange("p (h c) -> p h c", h=4),
                 s4.ap.unsqueeze(2).to_broadcast([128, 4, 64]), ALU.mult, R=[kk, s4], W=[kk])
            b2, b3 = yield from acq(2)
            S.mm(S.bank(b2)[:, 0:256], tri.ap[:, d, 0, :], sg, R=[tri, sga], W=[PS(b2)])
            S.mm(S.bank(b2)[:, 256:512], tri.ap[:, d, 1, :], sg, R=[tri, sga], W=[PS(b2)])
            S.mm(S.bank(b3)[:, 0:256], tri.ap[:, d, 2, :], sg, R=[tri, sga], W=[PS(b3)])
            for h in range(4):
                S.mm(S.bank(b3)[0:64, 256 + h:257 + h], sga.ap[:, h * 64:(h + 1) * 64], negc.ap, R=[sga, negc], W=[PS(b3)])
            kd = S.alloc([128, 256], F32, "kd")
            beta = S.alloc([128, 256], F32, "beta")
            S.stt("dve", t0.ap, a, -1.0, vec.ap[:, 1, :], ALU.add, ALU.mult, R=[sga, vec], W=[t0])
            S.stt("dve", kd.ap, t0.ap, 1.0, k, ALU.add, ALU.mult, R=[t0, rkv], W=[kd])
            S.tt("dve", beta.ap, kk.ap, a, ALU.mult, R=[kk, sga], W=[beta])
            yield
            WW = S.alloc([128, 512], F32, "WW")
            iW = S.alloc([128, 256], F32, "iW")
            Wrel = S.alloc([128, 256], F32, "Wrel")
            wcc = S.alloc([64, 4], F32, "wcc")
            S.act(WW.ap, S.bank(b2), AF.Exp, R=[PS(b2)], W=[WW])
            S.act(iW.ap, S.bank(b2)[:, 0:256], AF.Exp, R=[PS(b2)], W=[iW], scale=-1.0)
            S.act(Wrel.ap, S.bank(b3)[:, 0:256], AF.Exp, R=[PS(b3)], W=[Wrel])
            S.act(wcc.ap, S.bank(b3)[0:64, 256:260], AF.Exp, R=[PS(b3)], W=[wcc])
            BA.release([b2, b3])
            S.tt("dve", t0.ap, r, vec.ap[:, 2, :], ALU.mult, R=[rkv, vec], W=[t0])
            S.tt("dve", t0.ap, t0.ap, kd.ap, ALU.mult, R=[t0, kd], W=[t0])
            if b not in touched_b:
                touched_b.add(b)
                S.red("dve", bsum.ap[:, b, :], t0.ap.rearrange("p (h c) -> p h c", h=4), ALU.add, R=[t0], W=[bsum])
            else:
                S.red("dve", s4.ap, t0.ap.rearrange("p (h c) -> p h c", h=4), ALU.add, R=[t0], W=[s4])
                S.tt("dve", bsum.ap[:, b, :], bsum.ap[:, b, :], s4.ap, ALU.add, R=[bsum, s4], W=[bsum])
            yield
            TM = S.alloc([128, 4, 256], FTD, "TM")
            Btp = S.alloc([128, 256], STD, "Btp")
            Ktp = S.alloc([128, 256], STD, "Ktp")
            S.stt("dve", TM.ap[:, 0, :], kk.ap, -1.0, WW.ap[:, 256:512], ALU.mult, ALU.mult, R=[kk, WW], W=[TM])
            S.tt("dve", TM.ap[:, 1, :], r, WW.ap[:, 0:256], ALU.mult, R=[rkv, WW], W=[TM])
            S.tt("dve", TM.ap[:, 2, :], beta.ap, iW.ap, ALU.mult, R=[beta, iW], W=[TM])
            S.tt("dve", TM.ap[:, 3, :], kd.ap, iW.ap, ALU.mult, R=[kd, iW], W=[TM])
            S.tt("dve", Btp.ap, beta.ap, Wrel.ap, ALU.mult, R=[beta, Wrel], W=[Btp])
            S.tt("dve", Ktp.ap, kd.ap, Wrel.ap, ALU.mult, R=[kd, Wrel], W=[Ktp])
            S.release(kk, t0, s4, sga, WW, iW, Wrel, kd, beta)
            yield
            FT = S.alloc([64, 4, 4, 128], FTD, "FT")
            if FTD == BF16:
                for hp in range(2):
                    bt, = yield from acq(1)
                    for hh in range(2):
                        h = hp * 2 + hh
                        for q in range(4):
                            S.tr(S.bank(bt, BF16)[0:64, (hh * 4 + q) * 128:(hh * 4 + q + 1) * 128], TM.ap[:, q, h * 64:(h + 1) * 64],
                                 identb.ap, R=[TM, identb], W=[PS(bt)])
                    yield
                    S.evac(FT.ap[:, hp * 2:hp * 2 + 2, :, :], S.bank(bt, BF16)[0:64, :].rearrange("p (h q t) -> p h q t", h=2, q=4),
                           R=[PS(bt)], W=[FT])
                    BA.release([bt])
            else:
                for h in range(4):
                    bt, = yield from acq(1)
                    for q in range(4):
                        S.tr(S.bank(bt)[0:64, q * 128:(q + 1) * 128], TM.ap[:, q, h * 64:(h + 1) * 64], identf.ap,
                             R=[TM, identf], W=[PS(bt)])
                    yield
                    S.evac(FT.ap[:, h, :, :], S.bank(bt)[0:64, :].rearrange("p (q t) -> p q t", q=4), R=[PS(bt)], W=[FT])
                    BA.release([bt])
            S.release(TM)
            yield
            NAK = S.alloc([128, 4, 4, 128], STD, "NAK")
            NT0 = S.alloc([128, 4, 128], INVD, "NT0")
            alias_n0 = (INVD == BF16 and STD == BF16)
            N0 = None if alias_n0 else S.alloc([128, 4, 128], INVD, "N0")
            bnt, = yield from acq(1)
            for h in range(4):
                S.mm(S.bank(bnt)[:, h * 128:(h + 1) * 128], FT.ap[:, h, 0, :], FT.ap[:, h, 2, :], R=[FT], W=[PS(bnt)])
            for hp in range(2):
                pr = yield from acq(2, True)
                for hh in range(2):
                    h = hp * 2 + hh
                    bk = pr[hh]
                    ar = FT.ap[:, h, 0:2, :].rearrange("p a t -> p (a t)")
                    S.mm(S.bank(bk)[:, 0:256], FT.ap[:, h, 2, :], ar, R=[FT], W=[PS(bk)])
                    S.mm(S.bank(bk)[:, 256:512], FT.ap[:, h, 3, :], ar, R=[FT], W=[PS(bk)])
                yield
                if hp == 0:
                    S.tt("dve", NT0.ap, S.bank(bnt).rearrange("p (h t) -> p h t", h=4),
                         mskT.ap[:, d, :].unsqueeze(1).to_broadcast([128, 4, 128]), ALU.mult, R=[PS(bnt), mskT], W=[NT0])
                    BA.release([bnt])
                S.tt("dve", NAK.ap[:, hp * 2:hp * 2 + 2, :, :], S.banks(pr[0], 2).rearrange("p (h q t) -> p h q t", h=2, q=4),
                     msk.ap[:, d, :, :].unsqueeze(1).to_broadcast([128, 2, 4, 128]), ALU.mult, R=[PS(pr[0], 2), msk], W=[NAK])
                if not alias_n0:
                    for hh in range(2):
                        h = hp * 2 + hh
                        S.tt("dve", N0.ap[:, h, :], S.bank(pr[hh])[:, 0:128], msk.ap[:, d, 0, :], ALU.mult, R=[PS(pr[hh]), msk], W=[N0])
                BA.release(pr)
            yield
            NPa = S.alloc([128, 4, 2, 128], INVD, "NPa")
            NPb = S.alloc([128, 4, 2, 128], INVD, "NPb")
            NT1 = S.alloc([128, 4, 128], INVD, "NT1")
            NPs = [NPa, NPb]
            NTs = [NT0, NT1]
            ba, bb = yield from acq(2)
            for h in range(4):
                n0h = NAK.ap[:, h, 0, :] if alias_n0 else N0.ap[:, h, :]
                n0r = NAK if alias_n0 else N0
                S.mm(S.bank(ba)[:, h * 128:(h + 1) * 128], NT0.ap[:, h, :], n0h, R=[NT0, n0r], W=[PS(ba)])
                S.mm(S.bank(bb)[:, h * 128:(h + 1) * 128], n0h, NT0.ap[:, h, :], R=[NT0, n0r], W=[PS(bb)])
            n0all = NAK.ap[:, :, 0, :] if alias_n0 else N0.ap
            S.tt("dve", NPa.ap[:, :, 1, :], n0all, idI.ap.unsqueeze(1).to_broadcast([128, 4, 128]), ALU.add,
                 R=[NAK if alias_n0 else N0, idI], W=[NPa])
            yield
            S.cp("act", NPa.ap[:, :, 0, :], S.bank(ba).rearrange("p (h t) -> p h t", h=4), R=[PS(ba)], W=[NPa])
            S.cp("dve", NT1.ap, S.bank(bb).rearrange("p (h t) -> p h t", h=4), R=[PS(bb)], W=[NT1])
            BA.release([ba, bb])
            yield
            ci, ti = 0, 1
            for kl in range(1, 7):
                cur, nxt = NPs[ci], NPs[1 - ci]
                ntc, ntn = NTs[ti], NTs[1 - ti]
                need_n = kl <= 4
                need_nt = kl <= 5
                lvb = yield from acq_level(need_nt)
                b0 = lvb[0]
                for h in range(4):
                    bkh = b0 + h // 2
                    if need_n:
                        S.mm(S.banks(b0, 2)[:, h * 256:h * 256 + 256], ntc.ap[:, h, :],
                             cur.ap[:, h, :, :].rearrange("p a t -> p (a t)"), R=[ntc, cur], W=[PS(bkh)])
                    else:
                        S.mm(S.banks(b0, 2)[:, h * 256 + 128:h * 256 + 256], ntc.ap[:, h, :], cur.ap[:, h, 1, :],
                             R=[ntc, cur], W=[PS(bkh)])
                if need_nt:
                    bb = lvb[2]
                    for h in range(4):
                        S.mm(S.bank(bb)[:, h * 128:(h + 1) * 128], cur.ap[:, h, 0, :], ntc.ap[:, h, :], R=[ntc, cur], W=[PS(bb)])
                yield
                pv = S.banks(b0, 2).rearrange("p (h a t) -> p h a t", h=4, a=2)
                if need_n:
                    S.cp("act", nxt.ap[:, :, 0, :], pv[:, :, 0, :], R=[PS(b0, 2)], W=[nxt])
                S.tt("dve", nxt.ap[:, :, 1, :], cur.ap[:, :, 1, :], pv[:, :, 1, :], ALU.add, R=[cur, PS(b0, 2)], W=[nxt])
                if need_nt:
                    S.cp("act", ntn.ap, S.bank(bb).rearrange("p (h t) -> p h t", h=4), R=[PS(bb)], W=[ntn])
                    ti = 1 - ti
                BA.release(lvb)
                ci = 1 - ci
                yield
            Tt = S.alloc([128, 4, 128], STD, "Tb")
            S.cp("dve", Tt.ap, NPs[ci].ap[:, :, 1, :], R=[NPs[ci]], W=[Tt])
            S.release(NPa, NPb, NT0, NT1)
            if not alias_n0:
                S.release(N0)
            yield
            P_.update(FT=FT, NAK=NAK, T=Tt, Btp=Btp, Ktp=Ktp, wcc=wcc)

        cur_s = [0, 0]

        def step(i, d, P_):
            b = i if d == 0 else NB - 1 - i
            FT, NAK, Tt, Btp, Ktp, wcc = P_["FT"], P_["NAK"], P_["T"], P_["Btp"], P_["Ktp"], P_["wcc"]
            So = ST[d][cur_s[d]]
            Sn = ST[d][1 - cur_s[d]]
            cur_s[d] = 1 - cur_s[d]
            Sb = SB[d]
            if (d == 0 and b % 2 == 0) or (d == 1 and b % 2 == 1):
                j = b // 2
                if i == 0:
                    S.dma(So.ap, sinit[l, d, j].rearrange("k (h v) -> k h v", h=4), writes=[So])
                else:
                    si = S.alloc([64, 4, 64], F32, "si")
                    S.dma(si.ap, sinit[l, d, j].rearrange("k (h v) -> k h v", h=4), writes=[si])
                    S.stt("dve", So.ap, So.ap, keep.ap[:, 0:1], si.ap, ALU.mult, ALU.add, R=[So, keep, si], W=[So])
                    S.release(si)
            if FTD == BF16:
                S.cp("dve", Sb.ap, So.ap, R=[So], W=[Sb])
                Sm = Sb
            else:
                Sm = So
            yield

            def vh(h):
                if STD == F32:
                    return rkv.ap[:, b, 512 + h * 64:512 + (h + 1) * 64]
                return vb.ap[:, b, h * 64:(h + 1) * 64]
            bx, = yield from acq(1)
            for h in range(4):
                o = S.bank(bx)[:, h * 64:(h + 1) * 64]
                S.mm(o, FT.ap[:, h, 0, :], Sm.ap[:, h, :], start=True, stop=False, R=[FT, Sm], W=[PS(bx)])
                S.mm(o, NAK.ap[:, h, 2, :], vh(h), start=False, stop=True, R=[NAK, vb, rkv], W=[PS(bx)])
            yield
            XT = S.alloc([128, 256], STD, "XT")
            S.evac(XT.ap, S.bank(bx)[:, 0:256], R=[PS(bx)], W=[XT])
            BA.release([bx])
            yield
            bu, = yield from acq(1)
            for h in range(4):
                S.mm(S.bank(bu)[:, h * 64:(h + 1) * 64], Tt.ap[:, h, :], XT.ap[:, h * 64:(h + 1) * 64],
                     R=[Tt, XT], W=[PS(bu)])
            yield
            UT = S.alloc([128, 256], STD, "UT")
            S.evac(UT.ap, S.bank(bu)[:, 0:256], R=[PS(bu)], W=[UT])
            BA.release([bu])
            yield
            bs, bo = yield from acq(2)
            for h in range(4):
                o = S.bank(bs)[0:64, h * 64:(h + 1) * 64]
                S.mm(o, Btp.ap[:, h * 64:(h + 1) * 64], UT.ap[:, h * 64:(h + 1) * 64], start=True, stop=False,
                     R=[Btp, UT], W=[PS(bs)])
                S.mm(o, Ktp.ap[:, h * 64:(h + 1) * 64], vh(h), start=False, stop=True, R=[Ktp, vb, rkv], W=[PS(bs)])
            for h in range(4):
                o = S.bank(bo)[:, h * 64:(h + 1) * 64]
                S.mm(o, FT.ap[:, h, 1, :], Sm.ap[:, h, :], start=True, stop=False, R=[FT, Sm], W=[PS(bo)])
                S.mm(o, NAK.ap[:, h, 1, :], UT.ap[:, h * 64:(h + 1) * 64], start=False, stop=False, R=[NAK, UT], W=[PS(bo)])
                S.mm(o, NAK.ap[:, h, 3, :], vh(h), start=False, stop=True, R=[NAK, vb, rkv], W=[PS(bo)])
            yield
            if FLAGS.get('dwc', True):
                wsc = S.alloc([64, 4, 64], F32, "wsc")
                S.tt("dve", wsc.ap, So.ap, wcc.ap.unsqueeze(2).to_broadcast([64, 4, 64]), ALU.mult, R=[So, wcc], W=[wsc])
                S.tt("dve", Sn.ap, wsc.ap, S.bank(bs)[0:64, 0:256].rearrange("p (h v) -> p h v", h=4), ALU.add,
                     R=[wsc, PS(bs)], W=[Sn])
                S.release(wsc)
            else:
                for h in range(4):
                    S.stt("dve", Sn.ap[:, h, :], So.ap[:, h, :], wcc.ap[:, h:h + 1], S.bank(bs)[0:64, h * 64:(h + 1) * 64],
                          ALU.mult, ALU.add, R=[So, wcc, PS(bs)], W=[Sn])
            ores = osum.sub(b * 256, (b + 1) * 256)
            if b not in touched_o:
                touched_o.add(b)
                S.cp("act", osum.ap[:, b, :], S.bank(bo)[:, 0:256], R=[PS(bo)], W=[ores])
            else:
                S.tt("dve", osum.ap[:, b, :], osum.ap[:, b, :], S.bank(bo)[:, 0:256], ALU.add, R=[ores, PS(bo)], W=[ores])
            BA.release([bs, bo])
            if (d == 0 and b % 2 == 1) or (d == 1 and b % 2 == 0):
                S.extra_out.append(S.dma(s_out[l, d, b // 2].rearrange("k (h v) -> k h v", h=4), Sn.ap, reads=[Sn]))
            S.release(XT, UT, FT, NAK, Tt, Btp, Ktp, wcc)
            yield

        step_done = [-1]
        prep_out = {}

        def prep_task(i, d):
            while step_done[0] < i - 2:
                yield
            P_ = {}
            yield from prep(i if d == 0 else NB - 1 - i, d, P_)
            prep_out[(i, d)] = P_

        def steps_task():
            for i in range(NB):
                while (i, 0) not in prep_out or (i, 1) not in prep_out:
                    yield
                yield from interleave([step(i, 0, prep_out.pop((i, 0))), step(i, 1, prep_out.pop((i, 1)))], 2)
                step_done[0] = i

        yield from interleave([steps_task()] + [prep_task(i, d) for i in range(NB) for d in range(2)],
                              1 + FLAGS.get('rww', 4))

        for b in range(NB):
            o3 = osum.ap[:, b, :].rearrange("p (h c) -> p h c", h=4)
            m4 = S.alloc([128, 4], F32, "m4")
            cen = S.alloc([128, 4, 64], F32, "cen")
            sq = S.alloc([128, 4, 64], F32, "sq")
            S.red("dve", m4.ap, o3, ALU.add, R=[osum], W=[m4])
            S.ts("dve", m4.ap, m4.ap, -1.0 / 64, None, ALU.mult, R=[m4], W=[m4])
            S.tt("dve", cen.ap, o3, m4.ap.unsqueeze(2).to_broadcast([128, 4, 64]), ALU.add, R=[osum, m4], W=[cen])
            S.tt("dve", sq.ap, cen.ap, cen.ap, ALU.mult, R=[cen], W=[sq])
            S.red("dve", m4.ap, sq.ap, ALU.add, R=[sq], W=[m4])
            S.ts("dve", m4.ap, m4.ap, 1.0 / 64, GN_EPS, ALU.mult, ALU.add, R=[m4], W=[m4])
            yield
            rsqrt_inplace(m4)
            yield
            S.tt("dve", cen.ap, cen.ap, m4.ap.unsqueeze(2).to_broadcast([128, 4, 64]), ALU.mult, R=[cen, m4], W=[cen])
            c2d = cen.ap.rearrange("p h c -> p (h c)")
            S.tt("dve", c2d, c2d, vec.ap[:, 3, :], ALU.mult, R=[cen, vec], W=[cen])
            S.tt("dve", c2d, c2d, vec.ap[:, 4, :], ALU.add, R=[cen, vec], W=[cen])
            S.tt("dve", sq.ap, rkv.ap[:, b, 512:768].rearrange("p (h c) -> p h c", h=4),
                 bsum.ap[:, b, :].unsqueeze(2).to_broadcast([128, 4, 64]), ALU.mult, R=[rkv, bsum], W=[sq])
            S.tt("dve", cen.ap, cen.ap, sq.ap, ALU.add, R=[cen, sq], W=[cen])
            yield
            bk, = yield from acq(1)
            for c2 in range(2):
                S.tr(S.bank(bk)[:, c2 * 128:(c2 + 1) * 128], c2d[:, c2 * 128:(c2 + 1) * 128], identf.ap, R=[cen, identf], W=[PS(bk)])
            yield
            zsl = zs.ap[:, 2:4, b * 128:(b + 1) * 128]
            S.tt("dve", zsl, S.bank(bk)[:, 0:256].rearrange("p (c t) -> p c t", c=2), zsl, ALU.mult, R=[PS(bk), zs], W=[zs])
            BA.release([bk])
            S.release(m4, cen, sq)
            yield
        S.release(rows_hi, rows_lo, onesb1)
        S.release(vec, rows, ups, tri, msk, mskT, negc, vb, osum, bsum, ST[0][0], ST[0][1], ST[1][0], ST[1][1], SB[0], SB[1])

    def p1_setup(l):
        S.phase = 'L%d.P1' % l
        acol = S.alloc([128, KC], F32, "acol")
        S.stt("dve", acol.ap, modT[l].ap[:, 8:16], 1.0, gnT.ap[:, l, :], ALU.add, ALU.mult, R=[modT[l], gnT], W=[acol])
        At = S.alloc([128, D], F32, "At")
        Bt = S.alloc([128, D], F32, "Bt")
        bcast_tile(At, acol.ap, [acol])
        bcast_tile(Bt, modT[l].ap[:, 0:8], [modT[l]])
        hT = S.alloc([128, KC, NTOK], BF16, "hT")
        S.release(acol)
        return hT, At, Bt

    def sq_block(b, ss8, junk):
        S.act(junk.ap, Xh[0].ap[:, b, :], AF.Square, R=[Xh[0].sub(b * D, (b + 1) * D)], W=[junk, ss8], accum=ss8.ap[:, b:b + 1])

    def rstd_finish(ss8):
        S.ts("dve", ss8.ap, ss8.ap, 1.0 / D, NORM_EPS, ALU.mult, ALU.add, R=[ss8], W=[ss8])
        rsqrt_inplace(ss8)

    def p1_block(b, hT, At, Bt, r8):
        tmp = S.alloc([128, D], F32, "tmp")
        hb = S.alloc([128, D], BF16, "hb")
        S.stt("dve", tmp.ap, Xh[0].ap[:, b, :], r8.ap[:, b:b + 1], At.ap, ALU.mult, ALU.mult,
              R=[Xh[0].sub(b * D, (b + 1) * D), r8, At], W=[tmp])
        S.tt("dve", hb.ap, tmp.ap, Bt.ap, ALU.add, R=[tmp, Bt], W=[hb])
        S.release(tmp)
        yield
        bk = S.nbank()
        for kc in range(KC):
            S.tr(S.bank(bk, BF16)[:, kc * 128:(kc + 1) * 128], hb.ap[:, kc * 128:(kc + 1) * 128], identb.ap,
                 R=[hb, identb], W=[PS(bk)])
        S.release(hb)
        yield
        S.evac(hT.ap[:, :, b * 128:(b + 1) * 128], S.bank(bk, BF16).rearrange("p (k t) -> p k t", k=KC),
               R=[PS(bk)], W=[hT.sub(0, 4096)])
        yield

    def final_block(b, gf, r8):
        yo = S.alloc([128, D], F32, "yo")
        S.stt("dve", yo.ap, Xh[0].ap[:, b, :], r8.ap[:, b:b + 1], gf.ap, ALU.mult, ALU.mult,
              R=[Xh[0].sub(b * D, (b + 1) * D), r8, gf], W=[yo])
        out_tokens.append(S.dma(y_out[b * 128:(b + 1) * 128, :], yo.ap, reads=[yo]))
        S.release(yo)
        yield

    hT_next = [None]
    for l in range(DEPTH):
        if l == 0:
            ss8 = S.alloc([128, NB], F32, "ss8")
            junk = S.alloc([128, D], BF16, "junk")
            for b in range(NB):
                sq_block(b, ss8, junk)
            rstd_finish(ss8)
            hT, At_, Bt_ = p1_setup(0)
            for _ in interleave([p1_block(b, hT, At_, Bt_, ss8) for b in range(NB)], 3):
                pass
            S.release(At_, Bt_, ss8, junk)
            S.release(Xh[0])
        else:
            hT = hT_next[0]
        S.phase = 'L%d.P2' % l
        ring_alloc()
        zs = S.alloc([128, 8, NTOK], BF16, "zs")
        auT = S.alloc([128, 2, NTOK], BF16, "auT")
        av = S.alloc([128, NB, 256], BF16, "av")
        rkv = S.alloc([128, NB, 768], F32, "rkv")
        wdadT = S.alloc([128, NTOK], BF16, "wdadT")
        qT = S.alloc([128, 2, NTOK], BF16, "qT")
        kdT = S.alloc([128, 2, NTOK + 256], BF16, "kdT")
        Vd = S.alloc([128, NB + 2, 2, 128], BF16, "Vd")
        dpT = S.alloc([128, 2, NTOK], BF16, "dpT")
        ropc = S.alloc([128, NTOK], F32, "ropc")
        rops = S.alloc([128, NTOK], F32, "rops")
        permb = S.alloc([128, 128], BF16, "permb")
        S.dma(ropc.ap, ropec, writes=[ropc])
        S.dma(rops.ap, ropes, writes=[rops])
        S.dma(permb.ap, perm, writes=[permb], eng="pool")
        S.memset("dve", kdT.ap[:, :, 0:128], 0.0, W=[kdT])
        S.memset("dve", kdT.ap[:, :, NTOK + 128:NTOK + 256], 0.0, W=[kdT])
        S.memset("dve", Vd.ap[:, 0, :, :], 0.0, W=[Vd])
        S.memset("dve", Vd.ap[:, NB + 1, :, :], 0.0, W=[Vd])

        def proj_fm(wt, c0, evac_fn, M=128):
            for th in range(2):
                bk = S.nbank()
                for kc in range(KC):
                    S.mm(S.bank(bk)[0:M, :], wt[:, kc, c0:c0 + M], hT.ap[:, kc, th * 512:(th + 1) * 512],
                         start=(kc == 0), stop=(kc == KC - 1), R=[wt_res[0], hT], W=[PS(bk)])
                evac_fn(th, bk)

        def proj_tm(wt, ncols, evac_fn):
            for b2 in range(NB // 2):
                bk = S.nbank()
                for bb in range(2):
                    b = b2 * 2 + bb
                    for kc in range(KC):
                        S.mm(S.bank(bk)[:, bb * 256:bb * 256 + ncols], hT.ap[:, kc, b * 128:(b + 1) * 128],
                             wt[:, kc, 0:ncols], start=(kc == 0), stop=(kc == KC - 1), R=[wt_res[0], hT], W=[PS(bk)])
                evac_fn(b2, bk)

        def rope_evac(dst_fn):
            def f(th, bk):
                raw = S.alloc([128, 512], BF16, "raw")
                S.cp("act", raw.ap, S.bank(bk), R=[PS(bk)], W=[raw])
                b2 = S.nbank()
                S.mm(S.bank(b2), permb.ap, raw.ap, R=[permb, raw], W=[PS(b2)])
                t1 = S.alloc([128, 512], F32, "t1")
                t2 = S.alloc([128, 512], F32, "t2")
                S.tt("dve", t1.ap, S.bank(b2), rops.ap[:, th * 512:(th + 1) * 512], ALU.mult, R=[PS(b2), rops], W=[t1])
                S.tt("dve", t2.ap, raw.ap, ropc.ap[:, th * 512:(th + 1) * 512], ALU.mult, R=[raw, ropc], W=[t2])
                dst, dres = dst_fn(th)
                S.tt("dve", dst, t1.ap, t2.ap, ALU.add, R=[t1, t2], W=[dres])
                S.release(raw, t1, t2)
            return f

        wt_res = [None]
        chunks = [(C_AU, 256), (C_AV, 256), (C_R, 256), (C_K, 256), (C_V, 256), (C_WD, 128), (C_CQ, 256),
                  (C_CK, 256), (C_DP, 256), (C_Z, 256), (C_Z + 256, 256), (C_Z + 512, 256), (C_Z + 768, 256)]
        loaded = [load_w_in(l, *chunks[i]) for i in range(min(NRING - 1, len(chunks)))]
        for ci, (c0, ncols) in enumerate(chunks):
            if ci + NRING - 1 < len(chunks):
                loaded.append(load_w_in(l, *chunks[ci + NRING - 1]))
            wtile = loaded[ci]
            wt = wtile.ap
            wt_res[0] = wtile
            if c0 == C_AU:
                for mc in range(2):
                    proj_fm(wt, mc * 128, lambda th, bk, mc=mc: S.evac(
                        auT.ap[:, mc, th * 512:(th + 1) * 512], S.bank(bk), R=[PS(bk)], W=[auT]))
            elif c0 == C_AV:
                proj_tm(wt, 256, lambda b2, bk: S.evac(
                    av.ap[:, b2 * 2:b2 * 2 + 2, :], S.bank(bk).rearrange("p (b c) -> p b c", b=2), R=[PS(bk)], W=[av]))
            elif c0 in (C_R, C_K, C_V):
                j = (c0 - C_R) // 256
                proj_tm(wt, 256, lambda b2, bk, j=j: S.evac(
                    rkv.ap[:, b2 * 2:b2 * 2 + 2, j * 256:(j + 1) * 256], S.bank(bk).rearrange("p (b c) -> p b c", b=2),
                    R=[PS(bk)], W=[rkv]))
            elif c0 == C_WD:
                def ev_wd(th, bk):
                    S.act(wdadT.ap[0:64, th * 512:(th + 1) * 512], S.bank(bk)[0:64, :], AF.Tanh, R=[PS(bk)], W=[wdadT])
                    S.cp("dve", wdadT.ap[64:128, th * 512:(th + 1) * 512], S.bank(bk)[64:128, :], R=[PS(bk)], W=[wdadT])
                proj_fm(wt, 0, ev_wd)
            elif c0 == C_CQ:
                for kvh in range(2):
                    proj_fm(wt, kvh * 128, rope_evac(lambda th, kvh=kvh: (qT.ap[:, kvh, th * 512:(th + 1) * 512], qT)))
            elif c0 == C_CK:
                kvtm = S.alloc([128, NB, 256], F32, "kvtm")
                proj_tm(wt, 256, lambda b2, bk: S.evac(
                    kvtm.ap[:, b2 * 2:b2 * 2 + 2, :], S.bank(bk).rearrange("p (b c) -> p b c", b=2), R=[PS(bk)], W=[kvtm]))
                out_tokens.append(S.dma(k_out[l].rearrange("(n p) c -> p n c", p=128), kvtm.ap[:, :, 0:128], reads=[kvtm]))
                out_tokens.append(S.dma(v_out[l].rearrange("(n p) c -> p n c", p=128), kvtm.ap[:, :, 128:256], reads=[kvtm]))
                for b in range(NB):
                    S.cp("dve", Vd.ap[:, b + 1, :, :].rearrange("p k (u h) -> p k u h", u=2),
                         kvtm.ap[:, b, 128:256].rearrange("p (k h) -> p k h", k=2).unsqueeze(2).to_broadcast([128, 2, 2, 64]),
                         R=[kvtm], W=[Vd])
                wkd = S.alloc([128, KC, 2, 128], BF16, "wkd")
                S.cp("dve", wkd.ap.rearrange("p c k (u h) -> p c k u h", u=2),
                     wt[:, :, 0:128].rearrange("p c (k h) -> p c k h", k=2).unsqueeze(3).to_broadcast([128, KC, 2, 2, 64]),
                     R=[wtile], W=[wkd])
                wt_res[0] = wkd
                for kvh in range(2):
                    proj_fm(wkd.ap[:, :, kvh, :], 0,
                            rope_evac(lambda th, kvh=kvh: (kdT.ap[:, kvh, 128 + th * 512:128 + (th + 1) * 512], kdT)))
                S.release(kvtm, wkd)
            elif c0 == C_DP:
                for mc in range(2):
                    proj_fm(wt, mc * 128, lambda th, bk, mc=mc: S.evac(
                        dpT.ap[:, mc, th * 512:(th + 1) * 512], S.bank(bk), R=[PS(bk)], W=[dpT]))
            else:
                zi = (c0 - C_Z) // 256
                for mc in range(2):
                    proj_fm(wt, mc * 128, lambda th, bk, j=zi * 2 + mc: S.act(
                        zs.ap[:, j, th * 512:(th + 1) * 512], S.bank(bk), AF.Silu, R=[PS(bk)], W=[zs]))
        S.release(ropc, rops, permb)
        ring_free()

        S.phase = 'L%d.P3cd' % l
        S.evac_all_act = True
        _ab = [6]

        def abank():
            _ab[0] = 13 - _ab[0]
            return _ab[0]

        def chunkmix_gen():
            wsb = S.alloc([128, 4, 128], BF16, "wsb")
            bsb = S.alloc([128, 2, 128], F32, "bsb")
            S.dma(wsb.ap, w_sT[l].rearrange("j (g i) -> j g i", g=4), writes=[wsb], eng="pool")
            S.dma(bsb.ap, b_sT[l].rearrange("p (c i) -> p c i", c=2), writes=[bsb])
            for b in range(NB):
                for c2 in range(2):
                    bk, = yield from acq(1)
                    for gg in range(2):
                        S.mm(S.bank(bk)[:, gg * 128:(gg + 1) * 128], av.ap[:, b, c2 * 128:(c2 + 1) * 128],
                             wsb.ap[:, c2 * 2 + gg, :], R=[av, wsb], W=[PS(bk)])
                    yield
                    t = S.alloc([128, 128], F32, "cmt")
                    for gg in range(2):
                        rows = slice(gg * 64, (gg + 1) * 64)
                        S.tt("dve", t.ap[rows, :], S.bank(bk)[rows, gg * 128:(gg + 1) * 128], bsb.ap[rows, c2, :], ALU.add,
                             R=[PS(bk), bsb], W=[t])
                    S.tt("dve", t.ap, t.ap, auT.ap[:, c2, b * 128:(b + 1) * 128], ALU.mult, R=[t, auT], W=[t])
                    zsl = zs.ap[:, c2, b * 128:(b + 1) * 128]
                    S.tt("dve", zsl, t.ap, zsl, ALU.mult, R=[t, zs], W=[zs])
                    S.release(t)
                    BA.release([bk])
            S.release(wsb, bsb, auT, av)

        def pool_gen():
            pwb = S.alloc([128, 2, 256], BF16, "pwb")
            pmb = S.alloc([128, 8, 4, 128], BF16, "pmb")
            psc = S.alloc([128, 2], F32, "psc")
            S.dma(pwb.ap, poolw[l].rearrange("p (c d) -> p c d", c=2), writes=[pwb], eng="pool")
            S.dma(pmb.ap, poolm.rearrange("p (s g t) -> p s g t", s=8, g=4), writes=[pmb], eng="pool")
            S.dma(psc.ap, pscaleT[l], writes=[psc])
            ztm = S.alloc([128, NB, 256], BF16, "ztm")
            for b2 in range(NB // 2):
                bk, = yield from acq(1)
                for bb in range(2):
                    b = b2 * 2 + bb
                    for c2 in range(2):
                        S.mm(S.bank(bk)[:, bb * 256:(bb + 1) * 256], dpT.ap[:, c2, b * 128:(b + 1) * 128], pwb.ap[:, c2, :],
                             start=(c2 == 0), stop=(c2 == 1), R=[dpT, pwb], W=[PS(bk)])
                yield
                S.evac(ztm.ap[:, b2 * 2:b2 * 2 + 2, :], S.bank(bk).rearrange("p (b c) -> p b c", b=2), R=[PS(bk)], W=[ztm])
                BA.release([bk])
            for n in range(NB):
                dslot = 0 if n == 0 else (1 if n == NB - 1 else (2 if n % 2 == 1 else 3))
                nbrs = [(n, dslot)]
                if n >= 1:
                    nbrs.append((n - 1, 4 if n % 2 == 1 else 5))
                if n <= NB - 2:
                    nbrs.append((n + 1, 6 if n % 2 == 0 else 7))
                for c2 in range(2):
                    bk, = yield from acq(1)
                    for gg in range(2):
                        g = c2 * 2 + gg
                        for qi, (sb, slot) in enumerate(nbrs):
                            S.mm(S.bank(bk)[:, gg * 128:(gg + 1) * 128], ztm.ap[:, sb, c2 * 128:(c2 + 1) * 128],
                                 pmb.ap[:, slot, g, :], start=(qi == 0), stop=(qi == len(nbrs) - 1),
                                 R=[ztm, pmb], W=[PS(bk)])
                    yield
                    for gg in range(2):
                        rows = slice(gg * 64, (gg + 1) * 64)
                        zsl = zs.ap[rows, 6 + c2, n * 128:(n + 1) * 128]
                        S.stt("dve", zsl, S.bank(bk)[rows, gg * 128:(gg + 1) * 128], psc.ap[rows, c2:c2 + 1], zsl,
                              ALU.mult, ALU.mult, R=[PS(bk), psc, zs], W=[zs])
                    BA.release([bk])
            S.release(pwb, pmb, psc, ztm, dpT)


        def attn_gen():
            ABP_S = BankPool(4, 6)
            ABP_T = BankPool(6, 8)
            amb = S.alloc([128, 4, 896], BF16, "amb")
            ckb = S.alloc([128, 2, 512], BF16, "ckb")
            cvb = S.alloc([128, 4, 2, 128], BF16, "cvb")
            snk = S.alloc([128, 8], F32, "snk")
            S.dma(amb.ap, amask.rearrange("p (s k) -> p s k", s=4), writes=[amb], eng="pool")
            for u in range(2):
                S.dma(ckb.ap[u * 64:(u + 1) * 64, :, :], ckT[l].rearrange("k h t -> h k t"), writes=[ckb], eng="pool")
            S.dma(cvb.ap, cvd[l].rearrange("(n p) (k c) -> p n k c", p=128, k=2), writes=[cvb], eng="pool")
            S.dma(snk.ap, sinkb[l], writes=[snk])
            SCALE = 0.125
            yield

            def head(n, h):
                slot = 0 if n == 0 else (1 if n == NB - 1 else (2 if n % 2 == 1 else 3))
                kvh, g = h // 2, h % 2
                rows = slice(g * 64, (g + 1) * 64)
                hb_ = yield from acq(2, True)
                b0 = hb_[0]
                qsl = qT.ap[rows, kvh, n * 128:(n + 1) * 128]
                S.mm(S.bank(b0), qsl, ckb.ap[rows, kvh, :], start=True, stop=False, R=[qT, ckb], W=[PS(b0)])
                S.mm(S.bank(b0), identb.ap, amb.ap[:, slot, 0:512], start=False, stop=True, R=[identb, amb], W=[PS(b0)])
                S.mm(S.bank(b0 + 1)[:, 0:384], qsl, kdT.ap[rows, kvh, n * 128:n * 128 + 384], start=True, stop=False,
                     R=[qT, kdT], W=[PS(b0 + 1)])
                S.mm(S.bank(b0 + 1)[:, 0:384], identb.ap, amb.ap[:, slot, 512:896], start=False, stop=True,
                     R=[identb, amb], W=[PS(b0 + 1)])
                yield
                sc_ap = S.banks(b0, 2)[:, 0:896]
                st = S.alloc([128, 8], F32, "st")
                S.red("dve", st.ap[:, 0:1], sc_ap, ALU.max, R=[PS(b0, 2)], W=[st])
                S.ts("dve", st.ap[:, 1:2], st.ap[:, 0:1], -SCALE, None, ALU.mult, R=[st], W=[st])
                S.tt("dve", st.ap[:, 1:2], st.ap[:, 1:2], snk.ap[:, h:h + 1], ALU.min, R=[st, snk], W=[st])
                yield
                P = S.alloc([128, 896], BF16, "P")
                S.act(P.ap, sc_ap, AF.Exp, R=[PS(b0, 2), st], W=[P, st], bias=st.ap[:, 1:2], scale=SCALE,
                      accum=st.ap[:, 2:3])
                S.act(st.ap[:, 3:4], st.ap[:, 1:2], AF.Exp, R=[st, snk], W=[st], bias=snk.ap[:, 4 + h:5 + h], scale=1.0)
                BA.release(hb_)
                yield
                S.tt("dve", st.ap[:, 4:5], st.ap[:, 2:3], st.ap[:, 3:4], ALU.add, R=[st], W=[st])
                recip(st.ap[:, 5:6], st.ap[:, 4:5], [st], [st])
                Dg = S.alloc([128, 128], BF16, "Dg")
                S.ts("dve", Dg.ap, identf.ap, st.ap[:, 5:6], None, ALU.mult, R=[identf, st], W=[Dg])
                yield
                hb_ = yield from acq(2, True)
                b1 = hb_[0]
                for kb in range(7):
                    S.mm(S.banks(b1, 2)[:, kb * 128:(kb + 1) * 128], P.ap[:, kb * 128:(kb + 1) * 128], Dg.ap,
                         R=[P, Dg], W=[PS(b1 + kb // 4)])
                yield
                PT = S.alloc([128, 7, 128], BF16, "PT")
                S.cp("act", PT.ap, S.banks(b1, 2)[:, 0:896].rearrange("p (k q) -> p k q", k=7), R=[PS(b1, 2)], W=[PT])
                yield
                b2 = b1
                for kb in range(7):
                    if kb < 4:
                        vl, vr = cvb.ap[:, kb, kvh, :], cvb
                    else:
                        vl, vr = Vd.ap[:, n + kb - 4, kvh, :], Vd
                    S.mm(S.bank(b2)[:, 0:128], vl, PT.ap[:, kb, :], start=(kb == 0), stop=(kb == 6), R=[vr, PT], W=[PS(b2)])
                yield
                zsl = zs.ap[rows, 4 + kvh, n * 128:(n + 1) * 128]
                S.tt("dve", zsl, S.bank(b2)[rows, 0:128], zsl, ALU.mult, R=[PS(b2), zs], W=[zs])
                S.release(st, P, Dg, PT)
                BA.release(hb_)
                yield

            yield from interleave([head(n, h) for n in range(NB) for h in range(4)], FLAGS.get('attw', 2))
            S.release(amb, ckb, cvb, snk)

        def chain(*gs):
            for g_ in gs:
                yield from g_

        gens = []
        if FLAGS["mix_c"]:
            gens.append(chain(chunkmix_gen(), pool_gen(), attn_gen()))
        else:
            S.memset("dve", zs.ap[:, 4:6, :], 0.0, W=[zs])
        if FLAGS["mix_b"]:
            gens.append(rwkv_gen(l, zs, rkv, wdadT))
        else:
            S.memset("dve", zs.ap[:, 2:4, :], 0.0, W=[zs])
        if FLAGS.get('seqmix', False):
            for g_ in gens:
                rr([g_])
        else:
            rr(gens)
        S.evac_all_act = False
        S.release(qT, kdT, Vd)
        S.release(rkv, wdadT)

        S.phase = 'L%d.P4pre' % l
        S.phase = 'L%d.P4' % l
        Xh[0] = S.alloc([128, NB, D], F32, "X")
        if l == 0:
            S.dma(Xh[0].ap, xin.rearrange("(n p) d -> p n d", p=128), writes=[Xh[0]])
        else:
            S.dma(Xh[0].ap, xs.rearrange("(n p) d -> p n d", p=128), writes=[Xh[0]], after=list(xs_tok))
        ring_alloc()
        wub = S.alloc([128, 4, 2, D], BF16, "wub")
        S.dma(wub.ap, w_up[l].rearrange("n (c p) d -> p n c d", p=128), writes=[wub], eng="pool")
        mT = S.alloc([128, KC, NTOK], BF16, "mT")
        order = [(dcp, n) for dcp in range(4) for n in range(4)]
        loaded = [load_w_in(l, C_MG + n * 1024 + dcp * 256, 256) for (dcp, n) in order[:NRING - 1]]
        acc = None
        for oi, (dcp, n) in enumerate(order):
            if l == 0 and oi % 2 == 1 and mod1:
                try:
                    next(mod1[0])
                except StopIteration:
                    mod1.pop()
            if oi + NRING - 1 < len(order):
                d2, n2 = order[oi + NRING - 1]
                loaded.append(load_w_in(l, C_MG + n2 * 1024 + d2 * 256, 256))
            wtile = loaded[oi]
            if n == 0:
                acc = S.alloc([128, 2, NTOK], F32, "acc")
            for dd in range(2):
                dc = dcp * 2 + dd
                for th in range(2):
                    tsl = slice(th * 512, (th + 1) * 512)
                    bu = S.nbank()
                    for c2 in range(2):
                        S.mm(S.bank(bu), wub.ap[:, n, c2, dc * 128:(dc + 1) * 128], zs.ap[:, n * 2 + c2, tsl],
                             start=(c2 == 0), stop=(c2 == 1), R=[wub, zs], W=[PS(bu)])
                    bg = S.nbank()
                    for kc in range(KC):
                        S.mm(S.bank(bg), wtile.ap[:, kc, dd * 128:(dd + 1) * 128], hT.ap[:, kc, tsl],
                             start=(kc == 0), stop=(kc == KC - 1), R=[wtile, hT], W=[PS(bg)])
                    sg = S.alloc([128, 512], F32, "sg")
                    S.act(sg.ap, S.bank(bg), AF.Sigmoid, R=[PS(bg)], W=[sg])
                    asl = acc.ap[:, dd, tsl]
                    ares = acc.sub(dd * NTOK + th * 512, dd * NTOK + (th + 1) * 512)
                    if n == 0:
                        S.tt("dve", asl, S.bank(bu), sg.ap, ALU.mult, R=[PS(bu), sg], W=[ares])
                    else:
                        S.tt("dve", sg.ap, S.bank(bu), sg.ap, ALU.mult, R=[PS(bu), sg], W=[sg])
                        if n < 3:
                            S.tt("dve", asl, asl, sg.ap, ALU.add, R=[ares, sg], W=[ares])
                        else:
                            S.tt("dve", mT.ap[:, dc, tsl], asl, sg.ap, ALU.add, R=[ares, sg], W=[mT])
                    S.release(sg)
            if n == 3:
                S.release(acc)
        S.release(wub, zs, hT)
        ring_free()
        if l == 0:
            while mod1:
                try:
                    next(mod1[0])
                except StopIteration:
                    mod1.pop()

        S.phase = 'L%d.P5' % l
        wob = S.alloc([128, KC, D], BF16, "wob")
        S.dma(wob.ap, w_o[l].rearrange("(c p) d -> p c d", p=128), writes=[wob], eng="pool")
        Gt = S.alloc([128, D], F32, "Gt")
        bcast_tile(Gt, modT[l].ap[:, 16:24], [modT[l]])
        if l + 1 < DEPTH:
            hTn, At_, Bt_ = p1_setup(l + 1)
            S.phase = 'L%d.P5' % l
            nxt_gen = lambda b_: p1_block(b_, hTn, At_, Bt_, ss8)
        else:
            gf = S.alloc([128, D], F32, "gf")
            S.dma(gf.ap, g_fin_b, writes=[gf])
            nxt_gen = lambda b_: final_block(b_, gf, ss8)
        ss8 = S.alloc([128, NB], F32, "ss8")
        junk = S.alloc([128, D], BF16, "junk")
        for b in range(NB):
            for hh in range(2):
                bk = S.nbank()
                for dc in range(KC):
                    S.mm(S.bank(bk), mT.ap[:, dc, b * 128:(b + 1) * 128], wob.ap[:, dc, hh * 512:(hh + 1) * 512],
                         start=(dc == 0), stop=(dc == KC - 1), R=[mT, wob], W=[PS(bk)])
                t = S.alloc([128, 512], F32, "ot")
                S.tt("dve", t.ap, S.bank(bk), Gt.ap[:, hh * 512:(hh + 1) * 512], ALU.mult, R=[PS(bk), Gt], W=[t])
                xsl = Xh[0].ap[:, b, hh * 512:(hh + 1) * 512]
                xr = Xh[0].sub(b * D + hh * 512, b * D + (hh + 1) * 512)
                S.tt("dve", xsl, xsl, t.ap, ALU.add, R=[xr, t], W=[xr])
                S.release(t)
            if l + 1 < DEPTH:
                xs_tok.append(S.dma(xs[b * 128:(b + 1) * 128, :], Xh[0].ap[:, b, :], reads=[Xh[0].sub(b * D, (b + 1) * D)]))
            sq_block(b, ss8, junk)
        rstd_finish(ss8)
        for _ in interleave([nxt_gen(b) for b in range(NB)], 3):
            pass
        S.release(ss8, junk)
        S.release(wob, Gt, mT, Xh[0])
        if l + 1 < DEPTH:
            S.release(At_, Bt_)
            hT_next[0] = hTn
        else:
            S.release(gf)

    out_tokens.extend(S.extra_out)
    S.finish_wait(out_tokens)
    print("[kernel] ops=%d peak_words=%d starve=%s" % (S.nops, S.peak, STARVE))
    S.emit()
    build_program.marks = S.marks
    return nc


def kernel(**inp):
    f = lambda a: np.ascontiguousarray(np.asarray(a, dtype=np.float32))
    L = DEPTH
    x_prompt, x_sample = f(inp["x_prompt"]), f(inp["x_sample"])
    cache_k, cache_v, state = f(inp["cache_k"]), f(inp["cache_v"]), f(inp["state_rwkv"])
    c_s, c_ctx = f(inp["c"]), f(inp["c_ctx"])
    shared = {}
    shared.update(host_shared_consts())
    shared["w_mod"] = f(inp["w_mod"])
    shared["b_modT"] = f(f(inp["b_mod"]).reshape(L, 24, 128).transpose(0, 2, 1))
    shared["g_normT"] = f(f(inp["g_norm"]).reshape(L, 8, 128).transpose(0, 2, 1))
    shared["g_fin_b"] = f(np.broadcast_to(f(inp["g_final"])[None, :], (128, D)))
    shared["w_in"] = f(inp["w_in"])
    shared["w_sT"] = f(f(inp["w_s"]).transpose(0, 3, 1, 2).reshape(L, 128, 512))
    bs = f(inp["b_s"]).reshape(L, 2, 2, 128).transpose(0, 2, 1, 3)
    shared["b_sT"] = f(np.broadcast_to(bs[:, :, None, :, :], (L, 2, 64, 2, 128)).reshape(L, 128, 256))
    pw = f(inp["pool_w"])
    bd = np.zeros((L, 256, 256), np.float32)
    for g in range(4):
        bd[:, g * 64:(g + 1) * 64, g * 64:(g + 1) * 64] = pw[:, g]
    shared["poolw"] = f(bd.reshape(L, 2, 128, 256).transpose(0, 2, 1, 3).reshape(L, 128, 512))
    shared["pscaleT"] = f(f(inp["pool_scale"]).reshape(L, 2, 128).transpose(0, 2, 1))
    sk = f(inp["att_sink"])
    shared["sinkb"] = f(np.broadcast_to(np.concatenate([-sk, sk], axis=1)[:, None, :], (L, 128, 8)))
    shared["w_up"] = f(inp["w_up"])
    shared["w_o"] = f(inp["w_o"])
    vecs = np.stack([f(inp["rw_k_k"]), f(inp["rw_k_a"]), f(inp["rw_r_k"]).reshape(L, 256),
                     f(inp["rw_ln_g"]), f(inp["rw_ln_b"])], axis=1)
    shared["rwvec"] = f(np.broadcast_to(vecs[:, None, :, :], (L, 128, 5, 256)).reshape(L, 128, 1280))
    w0, a0 = f(inp["rw_w0"]), f(inp["rw_a0"])
    shared["rwrow"] = f(np.concatenate([w0[:, 0], w0[:, 1], a0[:, 0], a0[:, 1]], axis=1))
    wu, au = f(inp["rw_w_up"]), f(inp["rw_a_up"])
    ru = np.zeros((L, 128, 4, 256), np.float32)
    ru[:, 0:64, 0] = wu[:, 0]
    ru[:, 0:64, 1] = wu[:, 1]
    ru[:, 64:128, 2] = au[:, 0]
    ru[:, 64:128, 3] = au[:, 1]
    shared["rwup"] = f(ru.reshape(L, 128, 1024))
    cst = [host_consts(False), host_consts(True)]
    in_maps = []
    for c in range(8):
        m = dict(shared)
        samp = c >= 4
        m.update(cst[1 if samp else 0])
        if samp:
            b = c - 4
            m["xin"] = f(x_sample[b])
            cond = c_s[b]
            m["ckT"] = f(cache_k[b].transpose(0, 2, 3, 1))
            cv = cache_v[b]
            m["cvd"] = f(np.broadcast_to(cv[:, :, :, None, :], (L, 512, 2, 2, 64)).reshape(L, 512, 256))
            si = np.zeros((L, 2, 4, 64, 256), np.float32)
            st = state[b]
            stT = st.transpose(0, 1, 4, 2, 3).reshape(L, 2, 64, 256)
            si[:, 0, 0] = stT[:, 0]
            si[:, 1, 3] = stT[:, 1]
            m["sinit"] = si
            m["keepc"] = np.ones((64, 1), np.float32)
        else:
            m["xin"] = f(x_prompt[4 * c:4 * c + 4].reshape(NTOK, D))
            cond = c_ctx
            m["ckT"] = np.zeros((L, 2, 64, 512), np.float32)
            m["cvd"] = np.zeros((L, 512, 256), np.float32)
            m["sinit"] = np.zeros((L, 2, 4, 64, 256), np.float32)
            m["keepc"] = np.zeros((64, 1), np.float32)
        m["condT"] = f(cond.reshape(8, 128).T)
        in_maps.append(m)
    nc = build_program()
    res = run_bass_kernel_spmd(nc, in_maps, core_ids=list(range(8)))
    R = res.results
    y_prompt = np.concatenate([R[c]["y_out"].reshape(4, 256, D) for c in range(4)], axis=0)
    y_sample = np.stack([R[c]["y_out"] for c in range(4, 8)], axis=0)
    nk = np.concatenate([R[c]["k_out"].reshape(L, 4, 256, 2, 64).transpose(1, 0, 2, 3, 4) for c in range(4)], axis=0)
    nv = np.concatenate([R[c]["v_out"].reshape(L, 4, 256, 2, 64).transpose(1, 0, 2, 3, 4) for c in range(4)], axis=0)
    ns = np.concatenate([R[c]["s_out"].reshape(L, 2, 4, 64, 4, 64).transpose(2, 0, 1, 4, 5, 3) for c in range(4)], axis=0)
    return (f(y_prompt), f(y_sample), f(nk), f(nv), f(ns))
```

```python
import numpy as np
import concourse.bass as bass
import concourse.mybir as mybir

F32 = mybir.dt.float32
BF16 = mybir.dt.bfloat16
AF = mybir.ActivationFunctionType
ALU = mybir.AluOpType
AX = mybir.AxisListType

ENGS = ("pe", "dve", "act", "pool", "sp")
NDMA_SEMS = 32
SAME_ENGINE_WAR_SYNC = True


class IntervalMap:
    def __init__(self):
        self.segs = []

    def _split(self, lo, hi):
        out = []
        new = []
        cur = lo
        for s in self.segs:
            slo, shi, w, r = s
            if shi <= lo or slo >= hi:
                new.append(s)
                continue
            if slo < lo:
                new.append([slo, lo, w, dict(r)])
                slo = lo
            if shi > hi:
                tail = [hi, shi, w, dict(r)]
                shi = hi
            else:
                tail = None
            if cur < slo:
                g = [cur, slo, None, {}]
                new.append(g)
                out.append(g)
            m = [slo, shi, w, dict(r)]
            new.append(m)
            out.append(m)
            cur = shi
            if tail is not None:
                new.append(tail)
        if cur < hi:
            g = [cur, hi, None, {}]
            new.append(g)
            out.append(g)
        new.sort(key=lambda s: s[0])
        self.segs = new
        return out

    def access(self, lo, hi, is_write, eng, token):
        deps = []
        segs = self._split(lo, hi)
        for s in segs:
            if s[2] is not None:
                deps.append(("w", s[2]))
            if is_write:
                for e, t in s[3].items():
                    deps.append(("r", t))
        for s in segs:
            if is_write:
                s[2] = token
                s[3] = {}
            else:
                s[3][eng if token[0] == "eng" else token] = token
        merged = []
        for s in self.segs:
            if merged and merged[-1][1] == s[0] and merged[-1][2] == s[2] and merged[-1][3] == s[3]:
                merged[-1][1] = s[1]
            else:
                merged.append(s)
        self.segs = merged
        return deps


class Tile:
    def __init__(self, S, off, words, shape, dtype, name):
        self.S = S
        self.off = off
        self.words = words
        self.shape = list(shape)
        self.dtype = dtype
        self.name = name
        P = shape[0]
        free = int(np.prod(shape[1:]))
        base = S.arena[0:P, off:off + words]
        if dtype == BF16:
            base = base.bitcast(BF16)
        base = base[:, 0:free]
        if len(shape) > 2:
            names = " ".join("d%d" % i for i in range(len(shape) - 1))
            kw = {"d%d" % i: shape[i + 1] for i in range(len(shape) - 1)}
            base = base.rearrange("p (%s) -> p %s" % (names, names), **kw)
        self.ap = base
        self.res = ("sb", off, off + words)

    def sub(self, lo_words, hi_words):
        return ("sb", self.off + lo_words, self.off + hi_words)

    def __getitem__(self, k):
        return self.ap[k]


class Banks:
    def __init__(self, n=8):
        self.free = [True] * n
        self.stamp = [0] * n
        self.clock = 0

    def acquire(self, n=1, contiguous=False):
        idx = sorted([i for i, f in enumerate(self.free) if f], key=lambda i: self.stamp[i])
        if contiguous and n == 2:
            best = None
            for i in idx:
                if i + 1 < len(self.free) and self.free[i + 1]:
                    c = max(self.stamp[i], self.stamp[i + 1])
                    if best is None or c < best[0]:
                        best = (c, i)
            if best is None:
                return None
            i = best[1]
            self.free[i] = self.free[i + 1] = False
            return [i, i + 1]
        if len(idx) < n:
            return None
        r = idx[:n]
        for i in r:
            self.free[i] = False
        return r

    def release(self, banks):
        for b in banks:
            assert not self.free[b]
            self.free[b] = True
            self.clock += 1
            self.stamp[b] = self.clock


class Sched:
    def __init__(self, nc, arena_words):
        self.nc = nc
        self.arena_words = arena_words
        self._ctx = []
        g = nc.sbuf_tensor("arena", [128, arena_words], F32)
        self.arena = g.__enter__()
        self._ctx.append(g)
        g = nc.psum_tensor("psum", [128, 4096], F32)
        self.psum = g.__enter__()
        self._ctx.append(g)
        self.free = [(0, arena_words)]
        self.maps = {"sb": IntervalMap(), "ps": IntervalMap()}
        self.prog = {e: [] for e in ENGS}
        self.sems = {}
        for e in ENGS:
            g = nc.semaphore("sem_" + e)
            self.sems[e] = g.__enter__()
            self._ctx.append(g)
        self.dma_sems = []
        for i in range(NDMA_SEMS):
            g = nc.semaphore("dsem%d" % i)
            self.dma_sems.append(g.__enter__())
            self._ctx.append(g)
        self.dma_use = [0] * NDMA_SEMS
        self.dma_nexts = [0, 0]
        self.phase = ''
        self.marks = []
        self.waited = {e: {} for e in ENGS}
        self.peak = 0
        self.nops = 0

    def alloc(self, shape, dtype=F32, name="t"):
        P = shape[0]
        free = int(np.prod(shape[1:]))
        words = free if dtype == F32 else (free + 1) // 2
        words = (words + 7) // 8 * 8
        for i, (lo, hi) in enumerate(self.free):
            if hi - lo >= words:
                self.free[i] = (lo + words, hi)
                if self.free[i][0] == self.free[i][1]:
                    self.free.pop(i)
                used = self.arena_words - sum(h - l for l, h in self.free)
                self.peak = max(self.peak, used)
                return Tile(self, lo, words, shape, dtype, name)
        raise MemoryError("arena full allocating %s %s; free=%s" % (name, shape, self.free))

    def release(self, *tiles):
        for t in tiles:
            self.free.append((t.off, t.off + t.words))
        self.free.sort()
        m = []
        for lo, hi in self.free:
            if m and m[-1][1] == lo:
                m[-1] = (m[-1][0], hi)
            else:
                m.append((lo, hi))
        self.free = m

    def bank(self, b, dtype=F32):
        ap = self.psum[:, b * 512:(b + 1) * 512]
        if dtype == BF16:
            ap = ap.bitcast(BF16)
        return ap

    def banks(self, b, n):
        return self.psum[:, b * 512:(b + n) * 512]

    def _res(self, r):
        if isinstance(r, Tile):
            return r.res
        return r

    def op(self, eng, fn, reads=(), writes=(), dma=False, after=()):
        idx = len(self.prog[eng])
        if dma:
            half = NDMA_SEMS // 2
            grp = 1 if eng == "pool" else 0
            si = grp * half + self.dma_nexts[grp]
            self.dma_nexts[grp] = (self.dma_nexts[grp] + 1) % half
            self.dma_use[si] += 1
            val = 16 * self.dma_use[si]
            token = ("dma", si, val)
        else:
            token = ("eng", eng, idx)
        deps = []
        for r in reads:
            kind, lo, hi = self._res(r)
            if kind == "ps":
                deps += self.maps["ps"].access(lo, hi, True, eng, token)
            else:
                deps += self.maps[kind].access(lo, hi, False, eng, token)
        for r in writes:
            kind, lo, hi = self._res(r)
            deps += self.maps[kind].access(lo, hi, True, eng, token)
        waits = []
        w = self.waited[eng]
        if dma and val > 16:
            deps.append(("w", ("dma", si, val - 16)))
        for t_ in after:
            deps.append(("w", t_))
        for how, t in deps:
            if t == token:
                continue
            if t[0] == "eng":
                _, e2, j = t
                if e2 == eng and not dma:
                    if eng == "pe":
                        continue
                    if how != "w" and not SAME_ENGINE_WAR_SYNC:
                        continue
                key = ("eng", e2)
                if w.get(key, -1) >= j:
                    continue
                w[key] = j
                waits.append(t)
            else:
                _, s2, v = t
                key = ("dma", s2)
                if w.get(key, -1) >= v:
                    continue
                w[key] = v
                waits.append(t)
        best = {}
        for t in waits:
            k = (t[0], t[1])
            if k not in best or best[k][2] < t[2]:
                best[k] = t
        for t in best.values():
            if t[0] == "eng":
                self.prog[t[1]][t[2]]["signal"] = True
        self.prog[eng].append({"fn": fn, "waits": list(best.values()), "signal": False,
                               "dma": (si if dma else None), "phase": self.phase})
        self.nops += 1
        return token

    def dma(self, out_ap, in_ap, reads=(), writes=(), eng="sp", after=(), **kw):
        return self.op(eng, lambda e: e.dma_start(out=out_ap, in_=in_ap, **kw), reads, writes, dma=True, after=after)

    def finish_wait(self, tokens, eng="sp"):
        self.final_waits = (eng, tokens)


    def mm(self, out, lhsT, rhs, start=True, stop=True, R=(), W=()):
        return self.op("pe", lambda e: e.matmul(out, lhsT=lhsT, rhs=rhs, start=start, stop=stop), R, W)

    def tr(self, out, in_, ident, R=(), W=()):
        return self.op("pe", lambda e: e.transpose(out, in_, ident), R, W)

    def act(self, out, in_, func, R=(), W=(), bias=None, scale=None, accum=None):
        kw = {}
        if bias is not None:
            kw["bias"] = bias
        if scale is not None:
            kw["scale"] = scale
        if accum is not None:
            kw["accum_out"] = accum
        return self.op("act", lambda e: e.activation(out=out, in_=in_, func=func, **kw), R, W)

    def tt(self, eng, out, in0, in1, op, R=(), W=()):
        return self.op(eng, lambda e: e.tensor_tensor(out=out, in0=in0, in1=in1, op=op), R, W)

    def ts(self, eng, out, in0, s1, s2, op0, op1=None, R=(), W=()):
        if op1 is None:
            return self.op(eng, lambda e: e.tensor_scalar(out=out, in0=in0, scalar1=s1, scalar2=None, op0=op0), R, W)
        return self.op(eng, lambda e: e.tensor_scalar(out=out, in0=in0, scalar1=s1, scalar2=s2, op0=op0, op1=op1), R, W)

    def stt(self, eng, out, in0, scalar, in1, op0, op1, R=(), W=()):
        return self.op(eng, lambda e: e.scalar_tensor_tensor(out=out, in0=in0, scalar=scalar, in1=in1, op0=op0, op1=op1), R, W)

    def cp(self, eng, out, in_, R=(), W=()):
        if eng == "act":
            return self.op("act", lambda e: e.activation(out=out, in_=in_, func=AF.Copy), R, W)
        return self.op(eng, lambda e: e.tensor_copy(out=out, in_=in_), R, W)

    def red(self, eng, out, in_, op, R=(), W=()):
        return self.op(eng, lambda e: e.tensor_reduce(out=out, in_=in_, axis=AX.X, op=op), R, W)

    def memset(self, eng, ap, val, W=()):
        return self.op(eng, lambda e: e.memset(ap, val), (), W)

    def evac(self, out, in_, R=(), W=()):
        self._ev = getattr(self, "_ev", 0) + 1
        if getattr(self, "evac_all_act", False):
            return self.cp("act", out, in_, R, W)
        return self.cp("act" if self._ev % 2 else "dve", out, in_, R, W)

    def nbank(self, n=1):
        b = getattr(self, "_nb", 0)
        if b + n > 8:
            b = 0
        self._nb = (b + n) % 8
        return b

    def emit(self):
        nc = self.nc
        signum = {}
        for e in ENGS:
            c = 0
            for i, o in enumerate(self.prog[e]):
                if o["signal"] and o["dma"] is None:
                    c += 1
                    signum[(e, i)] = c
        print('[sched] signals:', {e: sum(1 for o in self.prog[e] if o['signal'] and o['dma'] is None) for e in ENGS}, 'ops:', {e: len(self.prog[e]) for e in ENGS})
        handles = {"pe": "tensor", "dve": "vector", "act": "scalar", "pool": "gpsimd", "sp": "sync"}
        final = getattr(self, "final_waits", None)

        def run(ename):
            def body(eng):
                for o in self.prog[ename]:
                    for t in o["waits"]:
                        if t[0] == "eng":
                            eng.wait_ge(self.sems[t[1]], signum[(t[1], t[2])])
                        else:
                            eng.wait_ge(self.dma_sems[t[1]], t[2])
                    inst = o["fn"](eng)
                    if o["phase"] != getattr(self, "_lastph_" + ename, None):
                        setattr(self, "_lastph_" + ename, o["phase"])
                        try:
                            self.marks.append((ename, o["phase"], inst.ins.name))
                        except Exception:
                            pass
                    if o["dma"] is not None:
                        inst.then_inc(self.dma_sems[o["dma"]], 16)
                    elif o["signal"]:
                        inst.then_inc(self.sems[ename], 1)
                if final is not None and final[0] == ename:
                    seen = {}
                    for t in final[1]:
                        seen[t[1]] = max(seen.get(t[1], 0), t[2])
                    for s, v in seen.items():
                        eng.wait_ge(self.dma_sems[s], v)
            return body

        with nc.Block() as block:
            block.tensor(run("pe"))
            block.vector(run("dve"))
            block.scalar(run("act"))
            block.gpsimd(run("pool"))
            block.sync(run("sp"))
        for g in reversed(self._ctx):
            g.__exit__(None, None, None)

from concourse.bass_utils import run_bass_kernel_spmd

D = 1024
NTOK = 1024
NB = 8
KC = 8
DEPTH = 2
IN_W = 7296
C_AU, C_AV, C_R, C_K, C_V, C_WD, C_CQ, C_CK, C_DP, C_Z, C_MG = 0, 256, 512, 768, 1024, 1280, 1408, 1664, 1920, 2176, 3200
NORM_EPS = 1e-6
GN_EPS = 64e-5
DECAY_C = float(np.exp(-0.5))
ARENA_WORDS = 52992
NEG = -1.0e30

FLAGS = {"mix_a": True, "mix_b": True, "mix_c": True, "mix_d": True, "ft32": False, "inv32": False, "attw": 3, "seqmix": False, "step32": False, "rww": 3}


def _pool_full(L):
    mats = []
    t = np.arange(L)
    for w in (2, 4, 8, 16):
        lo = np.clip(t - w // 2, 0, L)
        hi = np.clip(t - w // 2 + w, 0, L)
        M = np.zeros((L, L), np.float64)
        for i in range(L):
            M[i, lo[i]:hi[i]] = 1.0 / (hi[i] - lo[i])
        M -= np.eye(L)
        mats.append(M.T.astype(np.float32))
    return mats


def host_consts(is_sample):
    c = {}
    full = _pool_full(1024 if is_sample else 256)
    nblk = len(full[0]) // 128
    pm = np.zeros((128, 8, 4, 128), np.float32)

    def blk(g, sb, tb):
        return full[g][sb * 128:(sb + 1) * 128, tb * 128:(tb + 1) * 128]
    for g in range(4):
        S_ = blk(g, 0, 0)
        E_ = blk(g, nblk - 1, nblk - 1)
        if is_sample:
            I_ = blk(g, 1, 1)
            L_ = blk(g, 0, 1)
            R_ = blk(g, 2, 1)
            slots = [S_, E_, I_, I_, L_, L_, R_, R_]
        else:
            L_ = blk(g, 0, 1)
            R_ = blk(g, 1, 0)
            Z_ = np.zeros_like(L_)
            slots = [S_, E_, E_, S_, L_, Z_, R_, Z_]
        for si, m in enumerate(slots):
            pm[:, si, g, :] = m
    c["poolm"] = pm.reshape(128, 8 * 4 * 128)
    i = np.arange(128)[:, None]
    j = np.arange(128)[None, :]
    allv = np.zeros((128, 128), np.float32)
    inv = np.full((128, 128), NEG, np.float32)
    tril = np.where(j >= i, 0.0, NEG).astype(np.float32)
    trir = np.where(j <= i, 0.0, NEG).astype(np.float32)
    am = np.zeros((128, 4, 896), np.float32)
    if is_sample:
        loc = [[inv, allv, trir], [tril, allv, inv], [tril, allv, trir], [tril, allv, trir]]
        ctxv = 0.0
    else:
        loc = [[inv, allv, allv], [allv, allv, inv], [allv, allv, inv], [inv, allv, allv]]
        ctxv = NEG
    for s in range(4):
        am[:, s, 0:512] = ctxv
        am[:, s, 512:896] = np.concatenate(loc[s], axis=1)
    c["amask"] = am.reshape(128, 4 * 896)
    cos = np.ones((64, 1024), np.float32)
    sin = np.zeros((64, 1024), np.float32)
    if is_sample:
        pos = np.arange(1024)
        row = (pos // 64).astype(np.float32)
        col = (pos % 64).astype(np.float32)
        inv_f = (10000.0 ** (-np.arange(0, 32, 2, dtype=np.float32) / 32)).astype(np.float32)
        for hd in range(64):
            p = row if hd < 32 else col
            jj = hd % 32
            f = inv_f[jj % 16]
            ang = (p * f).astype(np.float32)
            cos[hd] = np.cos(ang)
            sin[hd] = (-np.sin(ang)) if jj < 16 else np.sin(ang)
    c["ropec"] = np.concatenate([cos, cos], 0)
    c["ropes"] = np.concatenate([sin, sin], 0)
    return c


def host_shared_consts():
    c = {}
    perm = np.zeros((128, 128), np.float32)
    for ii in range(128):
        jj = ii % 32
        partner = ii + 16 if jj < 16 else ii - 16
        perm[partner, ii] = 1.0
    c["perm"] = perm
    c["ident"] = np.eye(128, dtype=np.float32)
    s = np.arange(128)[:, None]
    t = np.arange(128)[None, :]
    tri = np.zeros((128, 2, 4, 128), np.float32)
    msk = np.zeros((128, 2, 4, 128), np.float32)
    mskT = np.zeros((128, 2, 128), np.float32)
    for d in range(2):
        incl = (s <= t) if d == 0 else (s >= t)
        strict = (s < t) if d == 0 else (s > t)
        tri[:, d, 0, :] = -DECAY_C * incl
        tri[:, d, 1, :] = -DECAY_C * strict
        tri[:, d, 2, :] = -DECAY_C * (1.0 - incl)
        msk[:, d, 0, :] = strict
        msk[:, d, 1, :] = incl
        msk[:, d, 2, :] = strict
        msk[:, d, 3, :] = incl
        mskT[:, d, :] = strict.T
    c["rwtri"] = tri.reshape(128, 2 * 4 * 128)
    c["rwmsk"] = msk.reshape(128, 2 * 4 * 128)
    c["rwmskT"] = mskT.reshape(128, 2 * 128)
    return c


def build_program():
    nc = bass.Bass("TRN2", target_bir_lowering=False)

    def din(name, shape):
        return nc.dram_tensor(name, list(shape), F32, kind="ExternalInput").ap()

    def dout(name, shape):
        return nc.dram_tensor(name, list(shape), F32, kind="ExternalOutput").ap()

    xin = din("xin", [NTOK, D])
    condT = din("condT", [128, KC])
    w_mod = din("w_mod", [DEPTH, D, 3 * D])
    b_modT = din("b_modT", [DEPTH, 128, 24])
    g_normT = din("g_normT", [DEPTH, 128, KC])
    g_fin_b = din("g_fin_b", [128, D])
    w_in = din("w_in", [DEPTH, D, IN_W])
    w_sT = din("w_sT", [DEPTH, 128, 4 * 128])
    b_sT = din("b_sT", [DEPTH, 128, 2 * 128])
    poolw = din("poolw", [DEPTH, 128, 2 * 256])
    pscaleT = din("pscaleT", [DEPTH, 128, 2])
    sinkb = din("sinkb", [DEPTH, 128, 8])
    w_up = din("w_up", [DEPTH, 4, 256, D])
    w_o = din("w_o", [DEPTH, D, D])
    ckT = din("ckT", [DEPTH, 2, 64, 512])
    cvd = din("cvd", [DEPTH, 512, 256])
    poolm = din("poolm", [128, 8 * 4 * 128])
    amask = din("amask", [128, 4 * 896])
    ropec = din("ropec", [128, NTOK])
    ropes = din("ropes", [128, NTOK])
    perm = din("perm", [128, 128])
    ident = din("ident", [128, 128])
    rwtri = din("rwtri", [128, 1024])
    rwmsk = din("rwmsk", [128, 1024])
    rwmskT = din("rwmskT", [128, 256])
    rwvec = din("rwvec", [DEPTH, 128, 5 * 256])
    rwrow = din("rwrow", [DEPTH, 4 * 256])
    rwup = din("rwup", [DEPTH, 128, 4 * 256])
    sinit = din("sinit", [DEPTH, 2, 4, 64, 256])
    keepc = din("keepc", [64, 1])

    y_out = dout("y_out", [NTOK, D])
    k_out = dout("k_out", [DEPTH, NTOK, 128])
    v_out = dout("v_out", [DEPTH, NTOK, 128])
    s_out = dout("s_out", [DEPTH, 2, 4, 64, 256])

    S = Sched(nc, ARENA_WORDS)
    S.extra_out = []
    out_tokens = []
    PS = lambda b, n=1: ("ps", b, b + n)

    Xh = [S.alloc([128, NB, D], F32, "X")]
    xs = nc.dram_tensor("xs_scratch", [NTOK, D], F32).ap()
    xs_tok = []
    identf = S.alloc([128, 128], F32, "identf")
    identb = S.alloc([128, 128], BF16, "identb")
    onesf = S.alloc([128, 128], F32, "onesf")
    modT = [S.alloc([128, 24], F32, "modT%d" % l) for l in range(DEPTH)]
    gnT = S.alloc([128, DEPTH, KC], F32, "gnT")
    keep = S.alloc([64, 1], F32, "keep")

    S.dma(Xh[0].ap, xin.rearrange("(n p) d -> p n d", p=128), writes=[Xh[0]])
    S.dma(identf.ap, ident, writes=[identf])
    S.dma(identb.ap, ident, writes=[identb], eng="pool")
    S.dma(gnT.ap, g_normT.rearrange("l p k -> p l k"), writes=[gnT])
    S.dma(keep.ap, keepc, writes=[keep])
    S.memset("dve", onesf.ap, 1.0, W=[onesf])
    mhalf = S.alloc([128, 8], F32, "mhalf")
    S.memset("dve", mhalf.ap, -0.5, W=[mhalf])

    S.phase = 'mod'
    sc = S.alloc([128, KC], F32, "sc")
    scb = S.alloc([128, KC], BF16, "scb")
    S.dma(sc.ap, condT, writes=[sc])
    S.act(scb.ap, sc.ap, AF.Silu, R=[sc], W=[scb])
    bmt = S.alloc([128, DEPTH, 24], F32, "bmt")
    S.dma(bmt.ap, b_modT.rearrange("l p c -> p l c"), writes=[bmt])
    S.release(sc)

    def mod_gen(l):
        wmb = [S.alloc([128, KC, 512], BF16, "wm%d" % i) for i in range(2)]
        mrow = S.alloc([1, 3 * D], F32, "mrow")
        wsrc = w_mod[l].rearrange("(kc p) c -> p kc c", p=128)
        S.dma(wmb[0].ap, wsrc[:, :, 0:512], writes=[wmb[0]], eng="pool")
        for ci in range(6):
            if ci + 1 < 6:
                S.dma(wmb[(ci + 1) % 2].ap, wsrc[:, :, (ci + 1) * 512:(ci + 2) * 512], writes=[wmb[(ci + 1) % 2]], eng="pool")
            yield
            wb = wmb[ci % 2]
            bk = S.nbank()
            for kc in range(KC):
                S.mm(S.bank(bk)[0:1, :], scb.ap[:, kc:kc + 1], wb.ap[:, kc, :], start=(kc == 0), stop=(kc == KC - 1),
                     R=[wb, scb], W=[PS(bk)])
            S.cp("act", mrow.ap[0:1, ci * 512:(ci + 1) * 512], S.bank(bk)[0:1, :], R=[PS(bk)], W=[mrow])
        yield
        bk = S.nbank()
        for j in range(24):
            S.mm(S.bank(bk)[:, j:j + 1], mrow.ap[0:1, j * 128:(j + 1) * 128], onesf.ap[0:1, 0:1], R=[mrow, onesf], W=[PS(bk)])
        S.tt("dve", modT[l].ap, S.bank(bk)[:, 0:24], bmt.ap[:, l, :], ALU.add, R=[PS(bk), bmt], W=[modT[l]])
        S.release(mrow, *wmb)

    for _ in mod_gen(0):
        pass
    mod1 = [mod_gen(1)]


    def recip(out, in_, R, W):
        S.op("dve", (lambda o, i: (lambda e: e.reciprocal(out=o, in_=i)))(out, in_), reads=R, writes=W)

    def rsqrt_inplace(t):
        P_, n_ = t.shape[0], int(np.prod(t.shape[1:]))
        S.tt("pool", t.ap, t.ap, mhalf.ap[0:P_, 0:n_], ALU.pow, R=[t, mhalf], W=[t])

    def bcast_tile(dst, colsrc, R):
        b0 = S.nbank(2)
        dg = S.alloc([128, 128], F32, "dg")
        for c in range(KC):
            S.ts("dve", dg.ap, identf.ap, colsrc[:, c:c + 1], None, ALU.mult, R=[identf] + R, W=[dg])
            S.mm(S.banks(b0, 2)[:, c * 128:(c + 1) * 128], onesf.ap, dg.ap, R=[onesf, dg], W=[PS(b0 + c // 4)])
        S.cp("act", dst.ap[:, 0:512], S.bank(b0), R=[PS(b0)], W=[dst.sub(0, 512)])
        S.cp("dve", dst.ap[:, 512:1024], S.bank(b0 + 1), R=[PS(b0 + 1)], W=[dst.sub(512, 1024)])
        S.release(dg)

    def rms_stats(xb, rstd, R):
        junk = S.alloc([128, D], BF16, "junk")
        ss = S.alloc([128, 1], F32, "ss")
        S.act(junk.ap, xb, AF.Square, R=R, W=[junk, ss], accum=ss.ap)
        S.ts("dve", rstd.ap, ss.ap, 1.0 / D, NORM_EPS, ALU.mult, ALU.add, R=[ss], W=[rstd])
        rsqrt_inplace(rstd)
        S.release(junk, ss)

    NRING = 4
    ring = []
    ring_i = [0]

    def ring_alloc():
        ring.extend(S.alloc([128, KC, 256], BF16, "ring%d" % i) for i in range(NRING))

    def ring_free():
        S.release(*ring)
        del ring[:]

    def load_w_in(l, c0, ncols):
        t = ring[ring_i[0] % NRING]
        ring_i[0] += 1
        S.dma(t.ap[:, :, 0:ncols], w_in[l].rearrange("(kc p) c -> p kc c", p=128)[:, :, c0:c0 + ncols],
              writes=[t], eng="pool")
        return t


    BA = Banks(8)
    STARVE = [0, 0]

    def acq(n=1, contiguous=False):
        while True:
            r_ = BA.acquire(n, contiguous)
            if r_ is not None:
                return r_
            STARVE[0] += 1
            yield

    def acq_level(need3):
        while True:
            r_ = BA.acquire(2, True)
            if r_ is not None:
                if not need3:
                    return r_
                r2 = BA.acquire(1)
                if r2 is not None:
                    return r_ + r2
                BA.release(r_)
            STARVE[1] += 1
            yield

    class BankPool:
        def __init__(self, lo, hi):
            self.lo, self.hi, self.cur = lo, hi, lo

        def get(self, n=1):
            if self.cur + n > self.hi:
                self.cur = self.lo
            b = self.cur
            self.cur += n
            if self.cur >= self.hi:
                self.cur = self.lo
            return b

    def rr(gens):
        act = list(gens)
        while act:
            for g in list(act):
                try:
                    next(g)
                except StopIteration:
                    act.remove(g)

    def interleave(gens, width):
        gens = list(gens)
        active = []
        while gens or active:
            while gens and len(active) < width:
                active.append(gens.pop(0))
            for g in list(active):
                try:
                    next(g)
                except StopIteration:
                    active.remove(g)
                yield

    def rwkv_gen(l, zs, rkv, wdadT):
        BP = BankPool(0, 4)
        FTD = F32 if FLAGS.get('ft32', True) else BF16
        INVD = F32 if FLAGS.get('inv32', True) else BF16
        STD = F32 if FLAGS.get('step32', False) else BF16
        idF = identf if FTD == F32 else identb
        idI = identf if INVD == F32 else identb
        vec = S.alloc([128, 5, 256], F32, "rwvec")
        rows = S.alloc([1, 4, 256], F32, "rwrows")
        ups = S.alloc([128, 4, 256], BF16, "rwups")
        rows_hi = S.alloc([1, 4, 256], BF16, "rwrows_hi")
        rows_lo = S.alloc([1, 4, 256], BF16, "rwrows_lo")
        onesb1 = S.alloc([1, 128], BF16, "onesb1")
        tri = S.alloc([128, 2, 4, 128], F32, "rwtri")
        msk = S.alloc([128, 2, 4, 128], BF16, "rwmsk")
        mskT = S.alloc([128, 2, 128], BF16, "rwmskT")
        negc = S.alloc([128, 1], F32, "negc")
        vb = S.alloc([128, NB, 256], BF16, "vb")
        S.dma(vec.ap, rwvec[l].rearrange("p (a c) -> p a c", a=5), writes=[vec])
        S.dma(rows.ap, rwrow[l:l + 1, :].rearrange("o (a c) -> o a c", a=4), writes=[rows])
        S.dma(ups.ap, rwup[l].rearrange("p (a c) -> p a c", a=4), writes=[ups], eng="pool")
        S.memset("dve", onesb1.ap, 1.0, W=[onesb1])
        S.cp("dve", rows_hi.ap, rows.ap, R=[rows], W=[rows_hi])
        S.tt("dve", rows.ap, rows.ap, rows_hi.ap, ALU.subtract, R=[rows, rows_hi], W=[rows])
        S.cp("dve", rows_lo.ap, rows.ap, R=[rows], W=[rows_lo])
        S.dma(tri.ap, rwtri.rearrange("p (d a t) -> p d a t", d=2, a=4), writes=[tri])
        S.dma(msk.ap, rwmsk.rearrange("p (d a t) -> p d a t", d=2, a=4), writes=[msk], eng="pool")
        S.dma(mskT.ap, rwmskT.rearrange("p (d t) -> p d t", d=2), writes=[mskT], eng="pool")
        S.memset("dve", negc.ap, -DECAY_C, W=[negc])
        S.cp("dve", vb.ap, rkv.ap[:, :, 512:768], R=[rkv], W=[vb])
        ones1 = onesf.ap[0:1, :]
        osum = S.alloc([128, NB, 256], F32, "osum")
        bsum = S.alloc([128, NB, 4], F32, "bsum")
        ST = [[S.alloc([64, 4, 64], F32, "ST%d%d" % (d, i)) for i in range(2)] for d in range(2)]
        SB = [S.alloc([64, 4, 64], BF16, "SB%d" % d) for d in range(2)]
        touched_o = set()
        touched_b = set()
        yield

        def prep(b, d, P_):
            r = rkv.ap[:, b, 0:256]
            k = rkv.ap[:, b, 256:512]
            kk = S.alloc([128, 256], F32, "kk")
            t0 = S.alloc([128, 256], F32, "t0")
            s4 = S.alloc([128, 4], F32, "s4")
            S.tt("dve", kk.ap, k, vec.ap[:, 0, :], ALU.mult, R=[rkv, vec], W=[kk])
            S.tt("dve", t0.ap, kk.ap, kk.ap, ALU.mult, R=[kk], W=[t0])
            S.red("dve", s4.ap, t0.ap.rearrange("p (h c) -> p h c", h=4), ALU.add, R=[t0], W=[s4])
            S.ts("dve", s4.ap, s4.ap, 1e-12, None, ALU.add, R=[s4], W=[s4])
            bk, = yield from acq(1)
            wsl = wdadT.ap[:, b * 128:(b + 1) * 128]
            for q in range(2):
                S.mm(S.bank(bk)[:, q * 256:(q + 1) * 256], wsl, ups.ap[:, q * 2 + d, :], start=True, stop=False,
                     R=[wdadT, ups], W=[PS(bk)])
                S.mm(S.bank(bk)[:, q * 256:(q + 1) * 256], onesb1.ap, rows_hi.ap[0:1, q * 2 + d, :], start=False, stop=False,
                     R=[onesb1, rows_hi], W=[PS(bk)])
                S.mm(S.bank(bk)[:, q * 256:(q + 1) * 256], onesb1.ap, rows_lo.ap[0:1, q * 2 + d, :], start=False, stop=True,
                     R=[onesb1, rows_lo], W=[PS(bk)])
            yield
            rsqrt_inplace(s4)
            sga = S.alloc([128, 512], F32, "sga")
            S.act(sga.ap, S.bank(bk), AF.Tanh, R=[PS(bk)], W=[sga], scale=0.5)
            BA.release([bk])
            S.ts("dve", sga.ap, sga.ap, 0.5, 0.5, ALU.mult, ALU.add, R=[sga], W=[sga])
            sg = sga.ap[:, 0:256]
            a = sga.ap[:, 256:512]
            yield
            S.tt("dve", kk.ap.rearrange("p (h c) -> p h c", h=4), kk.ap.rearrange("p (h c) -> p h c", h=4),
                 s4.ap.unsqueeze(2).to_broadcast([128, 4, 64]), ALU.mult, R=[kk, s4], W=[kk])
            b2, b3 = yield from acq(2)
            S.mm(S.bank(b2)[:, 0:256], tri.ap[:, d, 0, :], sg, R=[tri, sga], W=[PS(b2)])
            S.mm(S.bank(b2)[:, 256:512], tri.ap[:, d, 1, :], sg, R=[tri, sga], W=[PS(b2)])
            S.mm(S.bank(b3)[:, 0:256], tri.ap[:, d, 2, :], sg, R=[tri, sga], W=[PS(b3)])
            for h in range(4):
                S.mm(S.bank(b3)[0:64, 256 + h:257 + h], sga.ap[:, h * 64:(h + 1) * 64], negc.ap, R=[sga, negc], W=[PS(b3)])
            kd = S.alloc([128, 256], F32, "kd")
            beta = S.alloc([128, 256], F32, "beta")
            S.stt("dve", t0.ap, a, -1.0, vec.ap[:, 1, :], ALU.add, ALU.mult, R=[sga, vec], W=[t0])
            S.stt("dve", kd.ap, t0.ap, 1.0, k, ALU.add, ALU.mult, R=[t0, rkv], W=[kd])
            S.tt("dve", beta.ap, kk.ap, a, ALU.mult, R=[kk, sga], W=[beta])
            yield
            WW = S.alloc([128, 512], F32, "WW")
            iW = S.alloc([128, 256], F32, "iW")
            Wrel = S.alloc([128, 256], F32, "Wrel")
            wcc = S.alloc([64, 4], F32, "wcc")
            S.act(WW.ap, S.bank(b2), AF.Exp, R=[PS(b2)], W=[WW])
            S.act(iW.ap, S.bank(b2)[:, 0:256], AF.Exp, R=[PS(b2)], W=[iW], scale=-1.0)
            S.act(Wrel.ap, S.bank(b3)[:, 0:256], AF.Exp, R=[PS(b3)], W=[Wrel])
            S.act(wcc.ap, S.bank(b3)[0:64, 256:260], AF.Exp, R=[PS(b3)], W=[wcc])
            BA.release([b2, b3])
            S.tt("dve", t0.ap, r, vec.ap[:, 2, :], ALU.mult, R=[rkv, vec], W=[t0])
            S.tt("dve", t0.ap, t0.ap, kd.ap, ALU.mult, R=[t0, kd], W=[t0])
            if b not in touched_b:
                touched_b.add(b)
                S.red("dve", bsum.ap[:, b, :], t0.ap.rearrange("p (h c) -> p h c", h=4), ALU.add, R=[t0], W=[bsum])
            else:
                S.red("dve", s4.ap, t0.ap.rearrange("p (h c) -> p h c", h=4), ALU.add, R=[t0], W=[s4])
                S.tt("dve", bsum.ap[:, b, :], bsum.ap[:, b, :], s4.ap, ALU.add, R=[bsum, s4], W=[bsum])
            yield
            TM = S.alloc([128, 4, 256], FTD, "TM")
            Btp = S.alloc([128, 256], STD, "Btp")
            Ktp = S.alloc([128, 256], STD, "Ktp")
            S.stt("dve", TM.ap[:, 0, :], kk.ap, -1.0, WW.ap[:, 256:512], ALU.mult, ALU.mult, R=[kk, WW], W=[TM])
            S.tt("dve", TM.ap[:, 1, :], r, WW.ap[:, 0:256], ALU.mult, R=[rkv, WW], W=[TM])
            S.tt("dve", TM.ap[:, 2, :], beta.ap, iW.ap, ALU.mult, R=[beta, iW], W=[TM])
            S.tt("dve", TM.ap[:, 3, :], kd.ap, iW.ap, ALU.mult, R=[kd, iW], W=[TM])
            S.tt("dve", Btp.ap, beta.ap, Wrel.ap, ALU.mult, R=[beta, Wrel], W=[Btp])
            S.tt("dve", Ktp.ap, kd.ap, Wrel.ap, ALU.mult, R=[kd, Wrel], W=[Ktp])
            S.release(kk, t0, s4, sga, WW, iW, Wrel, kd, beta)
            yield
            FT = S.alloc([64, 4, 4, 128], FTD, "FT")
            if FTD == BF16:
                for hp in range(2):
                    bt, = yield from acq(1)
                    for hh in range(2):
                        h = hp * 2 + hh
                        for q in range(4):
                            S.tr(S.bank(bt, BF16)[0:64, (hh * 4 + q) * 128:(hh * 4 + q + 1) * 128], TM.ap[:, q, h * 64:(h + 1) * 64],
                                 identb.ap, R=[TM, identb], W=[PS(bt)])
                    yield
                    S.evac(FT.ap[:, hp * 2:hp * 2 + 2, :, :], S.bank(bt, BF16)[0:64, :].rearrange("p (h q t) -> p h q t", h=2, q=4),
                           R=[PS(bt)], W=[FT])
                    BA.release([bt])
            else:
                for h in range(4):
                    bt, = yield from acq(1)
                    for q in range(4):
                        S.tr(S.bank(bt)[0:64, q * 128:(q + 1) * 128], TM.ap[:, q, h * 64:(h + 1) * 64], identf.ap,
                             R=[TM, identf], W=[PS(bt)])
                    yield
                    S.evac(FT.ap[:, h, :, :], S.bank(bt)[0:64, :].rearrange("p (q t) -> p q t", q=4), R=[PS(bt)], W=[FT])
                    BA.release([bt])
            S.release(TM)
            yield
            NAK = S.alloc([128, 4, 4, 128], STD, "NAK")
            NT0 = S.alloc([128, 4, 128], INVD, "NT0")
            alias_n0 = (INVD == BF16 and STD == BF16)
            N0 = None if alias_n0 else S.alloc([128, 4, 128], INVD, "N0")
            bnt, = yield from acq(1)
            for h in range(4):
                S.mm(S.bank(bnt)[:, h * 128:(h + 1) * 128], FT.ap[:, h, 0, :], FT.ap[:, h, 2, :], R=[FT], W=[PS(bnt)])
            for h in range(4):
                bk, = yield from acq(1)
                ar = FT.ap[:, h, 0:2, :].rearrange("p a t -> p (a t)")
                S.mm(S.bank(bk)[:, 0:256], FT.ap[:, h, 2, :], ar, R=[FT], W=[PS(bk)])
                S.mm(S.bank(bk)[:, 256:512], FT.ap[:, h, 3, :], ar, R=[FT], W=[PS(bk)])
                if h == 0:
                    yield
                    S.tt("dve", NT0.ap, S.bank(bnt).rearrange("p (h t) -> p h t", h=4),
                         mskT.ap[:, d, :].unsqueeze(1).to_broadcast([128, 4, 128]), ALU.mult, R=[PS(bnt), mskT], W=[NT0])
                    BA.release([bnt])
                else:
                    yield
                S.tt("dve", NAK.ap[:, h, :, :], S.bank(bk).rearrange("p (q t) -> p q t", q=4), msk.ap[:, d, :, :], ALU.mult,
                     R=[PS(bk), msk], W=[NAK])
                if not alias_n0:
                    S.tt("dve", N0.ap[:, h, :], S.bank(bk)[:, 0:128], msk.ap[:, d, 0, :], ALU.mult, R=[PS(bk), msk], W=[N0])
                BA.release([bk])
            yield
            NPa = S.alloc([128, 4, 2, 128], INVD, "NPa")
            NPb = S.alloc([128, 4, 2, 128], INVD, "NPb")
            NT1 = S.alloc([128, 4, 128], INVD, "NT1")
            NPs = [NPa, NPb]
            NTs = [NT0, NT1]
            ba, bb = yield from acq(2)
            for h in range(4):
                n0h = NAK.ap[:, h, 0, :] if alias_n0 else N0.ap[:, h, :]
                n0r = NAK if alias_n0 else N0
                S.mm(S.bank(ba)[:, h * 128:(h + 1) * 128], NT0.ap[:, h, :], n0h, R=[NT0, n0r], W=[PS(ba)])
                S.mm(S.bank(bb)[:, h * 128:(h + 1) * 128], n0h, NT0.ap[:, h, :], R=[NT0, n0r], W=[PS(bb)])
            n0all = NAK.ap[:, :, 0, :] if alias_n0 else N0.ap
            S.tt("dve", NPa.ap[:, :, 1, :], n0all, idI.ap.unsqueeze(1).to_broadcast([128, 4, 128]), ALU.add,
                 R=[NAK if alias_n0 else N0, idI], W=[NPa])
            yield
            S.cp("act", NPa.ap[:, :, 0, :], S.bank(ba).rearrange("p (h t) -> p h t", h=4), R=[PS(ba)], W=[NPa])
            S.cp("dve", NT1.ap, S.bank(bb).rearrange("p (h t) -> p h t", h=4), R=[PS(bb)], W=[NT1])
            BA.release([ba, bb])
            yield
            ci, ti = 0, 1
            for kl in range(1, 7):
                cur, nxt = NPs[ci], NPs[1 - ci]
                ntc, ntn = NTs[ti], NTs[1 - ti]
                need_n = kl <= 4
                need_nt = kl <= 5
                lvb = yield from acq_level(need_nt)
                b0 = lvb[0]
                for h in range(4):
                    bkh = b0 + h // 2
                    if need_n:
                        S.mm(S.banks(b0, 2)[:, h * 256:h * 256 + 256], ntc.ap[:, h, :],
                             cur.ap[:, h, :, :].rearrange("p a t -> p (a t)"), R=[ntc, cur], W=[PS(bkh)])
                    else:
                        S.mm(S.banks(b0, 2)[:, h * 256 + 128:h * 256 + 256], ntc.ap[:, h, :], cur.ap[:, h, 1, :],
                             R=[ntc, cur], W=[PS(bkh)])
                if need_nt:
                    bb = lvb[2]
                    for h in range(4):
                        S.mm(S.bank(bb)[:, h * 128:(h + 1) * 128], cur.ap[:, h, 0, :], ntc.ap[:, h, :], R=[ntc, cur], W=[PS(bb)])
                yield
                pv = S.banks(b0, 2).rearrange("p (h a t) -> p h a t", h=4, a=2)
                if need_n:
                    S.cp("act", nxt.ap[:, :, 0, :], pv[:, :, 0, :], R=[PS(b0, 2)], W=[nxt])
                S.tt("dve", nxt.ap[:, :, 1, :], cur.ap[:, :, 1, :], pv[:, :, 1, :], ALU.add, R=[cur, PS(b0, 2)], W=[nxt])
                if need_nt:
                    S.cp("act", ntn.ap, S.bank(bb).rearrange("p (h t) -> p h t", h=4), R=[PS(bb)], W=[ntn])
                    ti = 1 - ti
                BA.release(lvb)
                ci = 1 - ci
                yield
            Tt = S.alloc([128, 4, 128], STD, "Tb")
            S.cp("dve", Tt.ap, NPs[ci].ap[:, :, 1, :], R=[NPs[ci]], W=[Tt])
            S.release(NPa, NPb, NT0, NT1)
            if not alias_n0:
                S.release(N0)
            yield
            P_.update(FT=FT, NAK=NAK, T=Tt, Btp=Btp, Ktp=Ktp, wcc=wcc)

        cur_s = [0, 0]

        def step(i, d, P_):
            b = i if d == 0 else NB - 1 - i
            FT, NAK, Tt, Btp, Ktp, wcc = P_["FT"], P_["NAK"], P_["T"], P_["Btp"], P_["Ktp"], P_["wcc"]
            So = ST[d][cur_s[d]]
            Sn = ST[d][1 - cur_s[d]]
            cur_s[d] = 1 - cur_s[d]
            Sb = SB[d]
            if (d == 0 and b % 2 == 0) or (d == 1 and b % 2 == 1):
                j = b // 2
                if i == 0:
                    S.dma(So.ap, sinit[l, d, j].rearrange("k (h v) -> k h v", h=4), writes=[So])
                else:
                    si = S.alloc([64, 4, 64], F32, "si")
                    S.dma(si.ap, sinit[l, d, j].rearrange("k (h v) -> k h v", h=4), writes=[si])
                    S.stt("dve", So.ap, So.ap, keep.ap[:, 0:1], si.ap, ALU.mult, ALU.add, R=[So, keep, si], W=[So])
                    S.release(si)
            if FTD == BF16:
                S.cp("dve", Sb.ap, So.ap, R=[So], W=[Sb])
                Sm = Sb
            else:
                Sm = So
            yield

            def vh(h):
                if STD == F32:
                    return rkv.ap[:, b, 512 + h * 64:512 + (h + 1) * 64]
                return vb.ap[:, b, h * 64:(h + 1) * 64]
            bx, = yield from acq(1)
            for h in range(4):
                o = S.bank(bx)[:, h * 64:(h + 1) * 64]
                S.mm(o, FT.ap[:, h, 0, :], Sm.ap[:, h, :], start=True, stop=False, R=[FT, Sm], W=[PS(bx)])
                S.mm(o, NAK.ap[:, h, 2, :], vh(h), start=False, stop=True, R=[NAK, vb, rkv], W=[PS(bx)])
            yield
            XT = S.alloc([128, 256], STD, "XT")
            S.evac(XT.ap, S.bank(bx)[:, 0:256], R=[PS(bx)], W=[XT])
            BA.release([bx])
            yield
            bu, = yield from acq(1)
            for h in range(4):
                S.mm(S.bank(bu)[:, h * 64:(h + 1) * 64], Tt.ap[:, h, :], XT.ap[:, h * 64:(h + 1) * 64],
                     R=[Tt, XT], W=[PS(bu)])
            yield
            UT = S.alloc([128, 256], STD, "UT")
            S.evac(UT.ap, S.bank(bu)[:, 0:256], R=[PS(bu)], W=[UT])
            BA.release([bu])
            yield
            bs, bo = yield from acq(2)
            for h in range(4):
                o = S.bank(bs)[0:64, h * 64:(h + 1) * 64]
                S.mm(o, Btp.ap[:, h * 64:(h + 1) * 64], UT.ap[:, h * 64:(h + 1) * 64], start=True, stop=False,
                     R=[Btp, UT], W=[PS(bs)])
                S.mm(o, Ktp.ap[:, h * 64:(h + 1) * 64], vh(h), start=False, stop=True, R=[Ktp, vb, rkv], W=[PS(bs)])
            for h in range(4):
                o = S.bank(bo)[:, h * 64:(h + 1) * 64]
                S.mm(o, FT.ap[:, h, 1, :], Sm.ap[:, h, :], start=True, stop=False, R=[FT, Sm], W=[PS(bo)])
                S.mm(o, NAK.ap[:, h, 1, :], UT.ap[:, h * 64:(h + 1) * 64], start=False, stop=False, R=[NAK, UT], W=[PS(bo)])
                S.mm(o, NAK.ap[:, h, 3, :], vh(h), start=False, stop=True, R=[NAK, vb, rkv], W=[PS(bo)])
            yield
            if FLAGS.get('dwc', False):
                wsc = S.alloc([64, 4, 64], F32, "wsc")
                S.tt("dve", wsc.ap, So.ap, wcc.ap.unsqueeze(2).to_broadcast([64, 4, 64]), ALU.mult, R=[So, wcc], W=[wsc])
                S.tt("dve", Sn.ap, wsc.ap, S.bank(bs)[0:64, 0:256].rearrange("p (h v) -> p h v", h=4), ALU.add,
                     R=[wsc, PS(bs)], W=[Sn])
                S.release(wsc)
            else:
                S.tt("dve", Sn.ap, So.ap, wcc.ap.unsqueeze(2).to_broadcast([64, 4, 64]), ALU.mult, R=[So, wcc], W=[Sn])
                S.tt("dve", Sn.ap, Sn.ap, S.bank(bs)[0:64, 0:256].rearrange("p (h v) -> p h v", h=4), ALU.add,
                     R=[Sn, PS(bs)], W=[Sn])
            ores = osum.sub(b * 256, (b + 1) * 256)
            if b not in touched_o:
                touched_o.add(b)
                S.cp("act", osum.ap[:, b, :], S.bank(bo)[:, 0:256], R=[PS(bo)], W=[ores])
            else:
                S.tt("dve", osum.ap[:, b, :], osum.ap[:, b, :], S.bank(bo)[:, 0:256], ALU.add, R=[ores, PS(bo)], W=[ores])
            BA.release([bs, bo])
            if (d == 0 and b % 2 == 1) or (d == 1 and b % 2 == 0):
                S.extra_out.append(S.dma(s_out[l, d, b // 2].rearrange("k (h v) -> k h v", h=4), Sn.ap, reads=[Sn]))
            S.release(XT, UT, FT, NAK, Tt, Btp, Ktp, wcc)
            yield

        step_done = [-1]
        prep_out = {}

        def prep_task(i, d):
            while step_done[0] < i - 2:
                yield
            P_ = {}
            yield from prep(i if d == 0 else NB - 1 - i, d, P_)
            prep_out[(i, d)] = P_

        def steps_task():
            for i in range(NB):
                while (i, 0) not in prep_out or (i, 1) not in prep_out:
                    yield
                yield from interleave([step(i, 0, prep_out.pop((i, 0))), step(i, 1, prep_out.pop((i, 1)))], 2)
                step_done[0] = i

        yield from interleave([steps_task()] + [prep_task(i, d) for i in range(NB) for d in range(2)],
                              1 + FLAGS.get('rww', 4))

        for b in range(NB):
            o3 = osum.ap[:, b, :].rearrange("p (h c) -> p h c", h=4)
            m4 = S.alloc([128, 4], F32, "m4")
            cen = S.alloc([128, 4, 64], F32, "cen")
            sq = S.alloc([128, 4, 64], F32, "sq")
            S.red("dve", m4.ap, o3, ALU.add, R=[osum], W=[m4])
            S.ts("dve", m4.ap, m4.ap, -1.0 / 64, None, ALU.mult, R=[m4], W=[m4])
            S.tt("dve", cen.ap, o3, m4.ap.unsqueeze(2).to_broadcast([128, 4, 64]), ALU.add, R=[osum, m4], W=[cen])
            S.tt("dve", sq.ap, cen.ap, cen.ap, ALU.mult, R=[cen], W=[sq])
            S.red("dve", m4.ap, sq.ap, ALU.add, R=[sq], W=[m4])
            S.ts("dve", m4.ap, m4.ap, 1.0 / 64, GN_EPS, ALU.mult, ALU.add, R=[m4], W=[m4])
            yield
            rsqrt_inplace(m4)
            yield
            S.tt("dve", cen.ap, cen.ap, m4.ap.unsqueeze(2).to_broadcast([128, 4, 64]), ALU.mult, R=[cen, m4], W=[cen])
            c2d = cen.ap.rearrange("p h c -> p (h c)")
            S.tt("dve", c2d, c2d, vec.ap[:, 3, :], ALU.mult, R=[cen, vec], W=[cen])
            S.tt("dve", c2d, c2d, vec.ap[:, 4, :], ALU.add, R=[cen, vec], W=[cen])
            S.tt("dve", sq.ap, rkv.ap[:, b, 512:768].rearrange("p (h c) -> p h c", h=4),
                 bsum.ap[:, b, :].unsqueeze(2).to_broadcast([128, 4, 64]), ALU.mult, R=[rkv, bsum], W=[sq])
            S.tt("dve", cen.ap, cen.ap, sq.ap, ALU.add, R=[cen, sq], W=[cen])
            yield
            bk, = yield from acq(1)
            for c2 in range(2):
                S.tr(S.bank(bk)[:, c2 * 128:(c2 + 1) * 128], c2d[:, c2 * 128:(c2 + 1) * 128], identf.ap, R=[cen, identf], W=[PS(bk)])
            yield
            zsl = zs.ap[:, 2:4, b * 128:(b + 1) * 128]
            S.tt("dve", zsl, S.bank(bk)[:, 0:256].rearrange("p (c t) -> p c t", c=2), zsl, ALU.mult, R=[PS(bk), zs], W=[zs])
            BA.release([bk])
            S.release(m4, cen, sq)
            yield
        S.release(rows_hi, rows_lo, onesb1)
        S.release(vec, rows, ups, tri, msk, mskT, negc, vb, osum, bsum, ST[0][0], ST[0][1], ST[1][0], ST[1][1], SB[0], SB[1])

    def p1_setup(l):
        S.phase = 'L%d.P1' % l
        acol = S.alloc([128, KC], F32, "acol")
        S.stt("dve", acol.ap, modT[l].ap[:, 8:16], 1.0, gnT.ap[:, l, :], ALU.add, ALU.mult, R=[modT[l], gnT], W=[acol])
        At = S.alloc([128, D], F32, "At")
        Bt = S.alloc([128, D], F32, "Bt")
        bcast_tile(At, acol.ap, [acol])
        bcast_tile(Bt, modT[l].ap[:, 0:8], [modT[l]])
        hT = S.alloc([128, KC, NTOK], BF16, "hT")
        S.release(acol)
        return hT, At, Bt

    def sq_block(b, ss8, junk):
        S.act(junk.ap, Xh[0].ap[:, b, :], AF.Square, R=[Xh[0].sub(b * D, (b + 1) * D)], W=[junk, ss8], accum=ss8.ap[:, b:b + 1])

    def rstd_finish(ss8):
        S.ts("dve", ss8.ap, ss8.ap, 1.0 / D, NORM_EPS, ALU.mult, ALU.add, R=[ss8], W=[ss8])
        rsqrt_inplace(ss8)

    def p1_block(b, hT, At, Bt, r8):
        tmp = S.alloc([128, D], F32, "tmp")
        hb = S.alloc([128, D], BF16, "hb")
        S.stt("dve", tmp.ap, Xh[0].ap[:, b, :], r8.ap[:, b:b + 1], At.ap, ALU.mult, ALU.mult,
              R=[Xh[0].sub(b * D, (b + 1) * D), r8, At], W=[tmp])
        S.tt("dve", hb.ap, tmp.ap, Bt.ap, ALU.add, R=[tmp, Bt], W=[hb])
        S.release(tmp)
        yield
        bk = S.nbank()
        for kc in range(KC):
            S.tr(S.bank(bk, BF16)[:, kc * 128:(kc + 1) * 128], hb.ap[:, kc * 128:(kc + 1) * 128], identb.ap,
                 R=[hb, identb], W=[PS(bk)])
        S.release(hb)
        yield
        S.evac(hT.ap[:, :, b * 128:(b + 1) * 128], S.bank(bk, BF16).rearrange("p (k t) -> p k t", k=KC),
               R=[PS(bk)], W=[hT.sub(0, 4096)])
        yield

    def final_block(b, gf, r8):
        yo = S.alloc([128, D], F32, "yo")
        S.stt("dve", yo.ap, Xh[0].ap[:, b, :], r8.ap[:, b:b + 1], gf.ap, ALU.mult, ALU.mult,
              R=[Xh[0].sub(b * D, (b + 1) * D), r8, gf], W=[yo])
        out_tokens.append(S.dma(y_out[b * 128:(b + 1) * 128, :], yo.ap, reads=[yo]))
        S.release(yo)
        yield

    hT_next = [None]
    for l in range(DEPTH):
        if l == 0:
            ss8 = S.alloc([128, NB], F32, "ss8")
            junk = S.alloc([128, D], BF16, "junk")
            for b in range(NB):
                sq_block(b, ss8, junk)
            rstd_finish(ss8)
            hT, At_, Bt_ = p1_setup(0)
            for _ in interleave([p1_block(b, hT, At_, Bt_, ss8) for b in range(NB)], 3):
                pass
            S.release(At_, Bt_, ss8, junk)
            S.release(Xh[0])
        else:
            hT = hT_next[0]
        S.phase = 'L%d.P2' % l
        ring_alloc()
        zs = S.alloc([128, 8, NTOK], BF16, "zs")
        auT = S.alloc([128, 2, NTOK], BF16, "auT")
        av = S.alloc([128, NB, 256], BF16, "av")
        rkv = S.alloc([128, NB, 768], F32, "rkv")
        wdadT = S.alloc([128, NTOK], BF16, "wdadT")
        qT = S.alloc([128, 2, NTOK], BF16, "qT")
        kdT = S.alloc([128, 2, NTOK + 256], BF16, "kdT")
        Vd = S.alloc([128, NB + 2, 2, 128], BF16, "Vd")
        dpT = S.alloc([128, 2, NTOK], BF16, "dpT")
        ropc = S.alloc([128, NTOK], F32, "ropc")
        rops = S.alloc([128, NTOK], F32, "rops")
        permb = S.alloc([128, 128], BF16, "permb")
        S.dma(ropc.ap, ropec, writes=[ropc])
        S.dma(rops.ap, ropes, writes=[rops])
        S.dma(permb.ap, perm, writes=[permb], eng="pool")
        S.memset("dve", kdT.ap[:, :, 0:128], 0.0, W=[kdT])
        S.memset("dve", kdT.ap[:, :, NTOK + 128:NTOK + 256], 0.0, W=[kdT])
        S.memset("dve", Vd.ap[:, 0, :, :], 0.0, W=[Vd])
        S.memset("dve", Vd.ap[:, NB + 1, :, :], 0.0, W=[Vd])

        def proj_fm(wt, c0, evac_fn, M=128):
            for th in range(2):
                bk = S.nbank()
                for kc in range(KC):
                    S.mm(S.bank(bk)[0:M, :], wt[:, kc, c0:c0 + M], hT.ap[:, kc, th * 512:(th + 1) * 512],
                         start=(kc == 0), stop=(kc == KC - 1), R=[wt_res[0], hT], W=[PS(bk)])
                evac_fn(th, bk)

        def proj_tm(wt, ncols, evac_fn):
            for b2 in range(NB // 2):
                bk = S.nbank()
                for bb in range(2):
                    b = b2 * 2 + bb
                    for kc in range(KC):
                        S.mm(S.bank(bk)[:, bb * 256:bb * 256 + ncols], hT.ap[:, kc, b * 128:(b + 1) * 128],
                             wt[:, kc, 0:ncols], start=(kc == 0), stop=(kc == KC - 1), R=[wt_res[0], hT], W=[PS(bk)])
                evac_fn(b2, bk)

        def rope_evac(dst_fn):
            def f(th, bk):
                raw = S.alloc([128, 512], BF16, "raw")
                S.cp("act", raw.ap, S.bank(bk), R=[PS(bk)], W=[raw])
                b2 = S.nbank()
                S.mm(S.bank(b2), permb.ap, raw.ap, R=[permb, raw], W=[PS(b2)])
                t1 = S.alloc([128, 512], F32, "t1")
                t2 = S.alloc([128, 512], F32, "t2")
                S.tt("dve", t1.ap, S.bank(b2), rops.ap[:, th * 512:(th + 1) * 512], ALU.mult, R=[PS(b2), rops], W=[t1])
                S.tt("dve", t2.ap, raw.ap, ropc.ap[:, th * 512:(th + 1) * 512], ALU.mult, R=[raw, ropc], W=[t2])
                dst, dres = dst_fn(th)
                S.tt("dve", dst, t1.ap, t2.ap, ALU.add, R=[t1, t2], W=[dres])
                S.release(raw, t1, t2)
            return f

        wt_res = [None]
        chunks = [(C_AU, 256), (C_AV, 256), (C_R, 256), (C_K, 256), (C_V, 256), (C_WD, 128), (C_CQ, 256),
                  (C_CK, 256), (C_DP, 256), (C_Z, 256), (C_Z + 256, 256), (C_Z + 512, 256), (C_Z + 768, 256)]
        loaded = [load_w_in(l, *chunks[i]) for i in range(min(NRING - 1, len(chunks)))]
        for ci, (c0, ncols) in enumerate(chunks):
            if ci + NRING - 1 < len(chunks):
                loaded.append(load_w_in(l, *chunks[ci + NRING - 1]))
            wtile = loaded[ci]
            wt = wtile.ap
            wt_res[0] = wtile
            if c0 == C_AU:
                for mc in range(2):
                    proj_fm(wt, mc * 128, lambda th, bk, mc=mc: S.evac(
                        auT.ap[:, mc, th * 512:(th + 1) * 512], S.bank(bk), R=[PS(bk)], W=[auT]))
            elif c0 == C_AV:
                proj_tm(wt, 256, lambda b2, bk: S.evac(
                    av.ap[:, b2 * 2:b2 * 2 + 2, :], S.bank(bk).rearrange("p (b c) -> p b c", b=2), R=[PS(bk)], W=[av]))
            elif c0 in (C_R, C_K, C_V):
                j = (c0 - C_R) // 256
                proj_tm(wt, 256, lambda b2, bk, j=j: S.evac(
                    rkv.ap[:, b2 * 2:b2 * 2 + 2, j * 256:(j + 1) * 256], S.bank(bk).rearrange("p (b c) -> p b c", b=2),
                    R=[PS(bk)], W=[rkv]))
            elif c0 == C_WD:
                def ev_wd(th, bk):
                    S.act(wdadT.ap[0:64, th * 512:(th + 1) * 512], S.bank(bk)[0:64, :], AF.Tanh, R=[PS(bk)], W=[wdadT])
                    S.cp("dve", wdadT.ap[64:128, th * 512:(th + 1) * 512], S.bank(bk)[64:128, :], R=[PS(bk)], W=[wdadT])
                proj_fm(wt, 0, ev_wd)
            elif c0 == C_CQ:
                for kvh in range(2):
                    proj_fm(wt, kvh * 128, rope_evac(lambda th, kvh=kvh: (qT.ap[:, kvh, th * 512:(th + 1) * 512], qT)))
            elif c0 == C_CK:
                kvtm = S.alloc([128, NB, 256], F32, "kvtm")
                proj_tm(wt, 256, lambda b2, bk: S.evac(
                    kvtm.ap[:, b2 * 2:b2 * 2 + 2, :], S.bank(bk).rearrange("p (b c) -> p b c", b=2), R=[PS(bk)], W=[kvtm]))
                out_tokens.append(S.dma(k_out[l].rearrange("(n p) c -> p n c", p=128), kvtm.ap[:, :, 0:128], reads=[kvtm]))
                out_tokens.append(S.dma(v_out[l].rearrange("(n p) c -> p n c", p=128), kvtm.ap[:, :, 128:256], reads=[kvtm]))
                for b in range(NB):
                    S.cp("dve", Vd.ap[:, b + 1, :, :].rearrange("p k (u h) -> p k u h", u=2),
                         kvtm.ap[:, b, 128:256].rearrange("p (k h) -> p k h", k=2).unsqueeze(2).to_broadcast([128, 2, 2, 64]),
                         R=[kvtm], W=[Vd])
                wkd = S.alloc([128, KC, 2, 128], BF16, "wkd")
                S.cp("dve", wkd.ap.rearrange("p c k (u h) -> p c k u h", u=2),
                     wt[:, :, 0:128].rearrange("p c (k h) -> p c k h", k=2).unsqueeze(3).to_broadcast([128, KC, 2, 2, 64]),
                     R=[wtile], W=[wkd])
                wt_res[0] = wkd
                for kvh in range(2):
                    proj_fm(wkd.ap[:, :, kvh, :], 0,
                            rope_evac(lambda th, kvh=kvh: (kdT.ap[:, kvh, 128 + th * 512:128 + (th + 1) * 512], kdT)))
                S.release(kvtm, wkd)
            elif c0 == C_DP:
                for mc in range(2):
                    proj_fm(wt, mc * 128, lambda th, bk, mc=mc: S.evac(
                        dpT.ap[:, mc, th * 512:(th + 1) * 512], S.bank(bk), R=[PS(bk)], W=[dpT]))
            else:
                zi = (c0 - C_Z) // 256
                for mc in range(2):
                    proj_fm(wt, mc * 128, lambda th, bk, j=zi * 2 + mc: S.act(
                        zs.ap[:, j, th * 512:(th + 1) * 512], S.bank(bk), AF.Silu, R=[PS(bk)], W=[zs]))
        S.release(ropc, rops, permb)
        ring_free()

        S.phase = 'L%d.P3cd' % l
        S.evac_all_act = True
        _ab = [6]

        def abank():
            _ab[0] = 13 - _ab[0]
            return _ab[0]

        def chunkmix_gen():
            wsb = S.alloc([128, 4, 128], BF16, "wsb")
            bsb = S.alloc([128, 2, 128], F32, "bsb")
            S.dma(wsb.ap, w_sT[l].rearrange("j (g i) -> j g i", g=4), writes=[wsb], eng="pool")
            S.dma(bsb.ap, b_sT[l].rearrange("p (c i) -> p c i", c=2), writes=[bsb])
            for b in range(NB):
                for c2 in range(2):
                    bk, = yield from acq(1)
                    for gg in range(2):
                        S.mm(S.bank(bk)[:, gg * 128:(gg + 1) * 128], av.ap[:, b, c2 * 128:(c2 + 1) * 128],
                             wsb.ap[:, c2 * 2 + gg, :], R=[av, wsb], W=[PS(bk)])
                    yield
                    t = S.alloc([128, 128], F32, "cmt")
                    for gg in range(2):
                        rows = slice(gg * 64, (gg + 1) * 64)
                        S.tt("dve", t.ap[rows, :], S.bank(bk)[rows, gg * 128:(gg + 1) * 128], bsb.ap[rows, c2, :], ALU.add,
                             R=[PS(bk), bsb], W=[t])
                    S.tt("dve", t.ap, t.ap, auT.ap[:, c2, b * 128:(b + 1) * 128], ALU.mult, R=[t, auT], W=[t])
                    zsl = zs.ap[:, c2, b * 128:(b + 1) * 128]
                    S.tt("dve", zsl, t.ap, zsl, ALU.mult, R=[t, zs], W=[zs])
                    S.release(t)
                    BA.release([bk])
            S.release(wsb, bsb, auT, av)

        def pool_gen():
            pwb = S.alloc([128, 2, 256], BF16, "pwb")
            pmb = S.alloc([128, 8, 4, 128], BF16, "pmb")
            psc = S.alloc([128, 2], F32, "psc")
            S.dma(pwb.ap, poolw[l].rearrange("p (c d) -> p c d", c=2), writes=[pwb], eng="pool")
            S.dma(pmb.ap, poolm.rearrange("p (s g t) -> p s g t", s=8, g=4), writes=[pmb], eng="pool")
            S.dma(psc.ap, pscaleT[l], writes=[psc])
            ztm = S.alloc([128, NB, 256], BF16, "ztm")
            for b2 in range(NB // 2):
                bk, = yield from acq(1)
                for bb in range(2):
                    b = b2 * 2 + bb
                    for c2 in range(2):
                        S.mm(S.bank(bk)[:, bb * 256:(bb + 1) * 256], dpT.ap[:, c2, b * 128:(b + 1) * 128], pwb.ap[:, c2, :],
                             start=(c2 == 0), stop=(c2 == 1), R=[dpT, pwb], W=[PS(bk)])
                yield
                S.evac(ztm.ap[:, b2 * 2:b2 * 2 + 2, :], S.bank(bk).rearrange("p (b c) -> p b c", b=2), R=[PS(bk)], W=[ztm])
                BA.release([bk])
            for n in range(NB):
                dslot = 0 if n == 0 else (1 if n == NB - 1 else (2 if n % 2 == 1 else 3))
                nbrs = [(n, dslot)]
                if n >= 1:
                    nbrs.append((n - 1, 4 if n % 2 == 1 else 5))
                if n <= NB - 2:
                    nbrs.append((n + 1, 6 if n % 2 == 0 else 7))
                for c2 in range(2):
                    bk, = yield from acq(1)
                    for gg in range(2):
                        g = c2 * 2 + gg
                        for qi, (sb, slot) in enumerate(nbrs):
                            S.mm(S.bank(bk)[:, gg * 128:(gg + 1) * 128], ztm.ap[:, sb, c2 * 128:(c2 + 1) * 128],
                                 pmb.ap[:, slot, g, :], start=(qi == 0), stop=(qi == len(nbrs) - 1),
                                 R=[ztm, pmb], W=[PS(bk)])
                    yield
                    for gg in range(2):
                        rows = slice(gg * 64, (gg + 1) * 64)
                        zsl = zs.ap[rows, 6 + c2, n * 128:(n + 1) * 128]
                        S.stt("dve", zsl, S.bank(bk)[rows, gg * 128:(gg + 1) * 128], psc.ap[rows, c2:c2 + 1], zsl,
                              ALU.mult, ALU.mult, R=[PS(bk), psc, zs], W=[zs])
                    BA.release([bk])
            S.release(pwb, pmb, psc, ztm, dpT)


        def attn_gen():
            ABP_S = BankPool(4, 6)
            ABP_T = BankPool(6, 8)
            amb = S.alloc([128, 4, 896], BF16, "amb")
            ckb = S.alloc([128, 2, 512], BF16, "ckb")
            cvb = S.alloc([128, 4, 2, 128], BF16, "cvb")
            snk = S.alloc([128, 8], F32, "snk")
            S.dma(amb.ap, amask.rearrange("p (s k) -> p s k", s=4), writes=[amb], eng="pool")
            for u in range(2):
                S.dma(ckb.ap[u * 64:(u + 1) * 64, :, :], ckT[l].rearrange("k h t -> h k t"), writes=[ckb], eng="pool")
            S.dma(cvb.ap, cvd[l].rearrange("(n p) (k c) -> p n k c", p=128, k=2), writes=[cvb], eng="pool")
            S.dma(snk.ap, sinkb[l], writes=[snk])
            SCALE = 0.125
            yield

            def head(n, h):
                slot = 0 if n == 0 else (1 if n == NB - 1 else (2 if n % 2 == 1 else 3))
                kvh, g = h // 2, h % 2
                rows = slice(g * 64, (g + 1) * 64)
                hb_ = yield from acq(2, True)
                b0 = hb_[0]
                qsl = qT.ap[rows, kvh, n * 128:(n + 1) * 128]
                S.mm(S.bank(b0), qsl, ckb.ap[rows, kvh, :], start=True, stop=False, R=[qT, ckb], W=[PS(b0)])
                S.mm(S.bank(b0), identb.ap, amb.ap[:, slot, 0:512], start=False, stop=True, R=[identb, amb], W=[PS(b0)])
                S.mm(S.bank(b0 + 1)[:, 0:384], qsl, kdT.ap[rows, kvh, n * 128:n * 128 + 384], start=True, stop=False,
                     R=[qT, kdT], W=[PS(b0 + 1)])
                S.mm(S.bank(b0 + 1)[:, 0:384], identb.ap, amb.ap[:, slot, 512:896], start=False, stop=True,
                     R=[identb, amb], W=[PS(b0 + 1)])
                yield
                sc_ap = S.banks(b0, 2)[:, 0:896]
                st = S.alloc([128, 8], F32, "st")
                S.red("dve", st.ap[:, 0:1], sc_ap, ALU.max, R=[PS(b0, 2)], W=[st])
                S.ts("dve", st.ap[:, 1:2], st.ap[:, 0:1], -SCALE, None, ALU.mult, R=[st], W=[st])
                S.tt("dve", st.ap[:, 1:2], st.ap[:, 1:2], snk.ap[:, h:h + 1], ALU.min, R=[st, snk], W=[st])
                yield
                P = S.alloc([128, 896], BF16, "P")
                S.act(P.ap, sc_ap, AF.Exp, R=[PS(b0, 2), st], W=[P, st], bias=st.ap[:, 1:2], scale=SCALE,
                      accum=st.ap[:, 2:3])
                S.act(st.ap[:, 3:4], st.ap[:, 1:2], AF.Exp, R=[st, snk], W=[st], bias=snk.ap[:, 4 + h:5 + h], scale=1.0)
                BA.release(hb_)
                yield
                S.tt("dve", st.ap[:, 4:5], st.ap[:, 2:3], st.ap[:, 3:4], ALU.add, R=[st], W=[st])
                recip(st.ap[:, 5:6], st.ap[:, 4:5], [st], [st])
                Dg = S.alloc([128, 128], BF16, "Dg")
                S.ts("dve", Dg.ap, identf.ap, st.ap[:, 5:6], None, ALU.mult, R=[identf, st], W=[Dg])
                yield
                hb_ = yield from acq(2, True)
                b1 = hb_[0]
                for kb in range(7):
                    S.mm(S.banks(b1, 2)[:, kb * 128:(kb + 1) * 128], P.ap[:, kb * 128:(kb + 1) * 128], Dg.ap,
                         R=[P, Dg], W=[PS(b1 + kb // 4)])
                yield
                PT = S.alloc([128, 7, 128], BF16, "PT")
                S.cp("act", PT.ap, S.banks(b1, 2)[:, 0:896].rearrange("p (k q) -> p k q", k=7), R=[PS(b1, 2)], W=[PT])
                yield
                b2 = b1
                for kb in range(7):
                    if kb < 4:
                        vl, vr = cvb.ap[:, kb, kvh, :], cvb
                    else:
                        vl, vr = Vd.ap[:, n + kb - 4, kvh, :], Vd
                    S.mm(S.bank(b2)[:, 0:128], vl, PT.ap[:, kb, :], start=(kb == 0), stop=(kb == 6), R=[vr, PT], W=[PS(b2)])
                yield
                zsl = zs.ap[rows, 4 + kvh, n * 128:(n + 1) * 128]
                S.tt("dve", zsl, S.bank(b2)[rows, 0:128], zsl, ALU.mult, R=[PS(b2), zs], W=[zs])
                S.release(st, P, Dg, PT)
                BA.release(hb_)
                yield

            yield from interleave([head(n, h) for n in range(NB) for h in range(4)], FLAGS.get('attw', 2))
            S.release(amb, ckb, cvb, snk)

        def chain(*gs):
            for g_ in gs:
                yield from g_

        gens = []
        if FLAGS["mix_c"]:
            gens.append(chain(chunkmix_gen(), pool_gen(), attn_gen()))
        else:
            S.memset("dve", zs.ap[:, 4:6, :], 0.0, W=[zs])
        if FLAGS["mix_b"]:
            gens.append(rwkv_gen(l, zs, rkv, wdadT))
        else:
            S.memset("dve", zs.ap[:, 2:4, :], 0.0, W=[zs])
        if FLAGS.get('seqmix', False):
            for g_ in gens:
                rr([g_])
        else:
            rr(gens)
        S.evac_all_act = False
        S.release(qT, kdT, Vd)
        S.release(rkv, wdadT)

        S.phase = 'L%d.P4pre' % l
        S.phase = 'L%d.P4' % l
        Xh[0] = S.alloc([128, NB, D], F32, "X")
        if l == 0:
            S.dma(Xh[0].ap, xin.rearrange("(n p) d -> p n d", p=128), writes=[Xh[0]])
        else:
            S.dma(Xh[0].ap, xs.rearrange("(n p) d -> p n d", p=128), writes=[Xh[0]], after=list(xs_tok))
        ring_alloc()
        wub = S.alloc([128, 4, 2, D], BF16, "wub")
        S.dma(wub.ap, w_up[l].rearrange("n (c p) d -> p n c d", p=128), writes=[wub], eng="pool")
        mT = S.alloc([128, KC, NTOK], BF16, "mT")
        order = [(dcp, n) for dcp in range(4) for n in range(4)]
        loaded = [load_w_in(l, C_MG + n * 1024 + dcp * 256, 256) for (dcp, n) in order[:NRING - 1]]
        acc = None
        for oi, (dcp, n) in enumerate(order):
            if l == 0 and oi % 2 == 1 and mod1:
                try:
                    next(mod1[0])
                except StopIteration:
                    mod1.pop()
            if oi + NRING - 1 < len(order):
                d2, n2 = order[oi + NRING - 1]
                loaded.append(load_w_in(l, C_MG + n2 * 1024 + d2 * 256, 256))
            wtile = loaded[oi]
            if n == 0:
                acc = S.alloc([128, 2, NTOK], F32, "acc")
            for dd in range(2):
                dc = dcp * 2 + dd
                for th in range(2):
                    tsl = slice(th * 512, (th + 1) * 512)
                    bu = S.nbank()
                    for c2 in range(2):
                        S.mm(S.bank(bu), wub.ap[:, n, c2, dc * 128:(dc + 1) * 128], zs.ap[:, n * 2 + c2, tsl],
                             start=(c2 == 0), stop=(c2 == 1), R=[wub, zs], W=[PS(bu)])
                    bg = S.nbank()
                    for kc in range(KC):
                        S.mm(S.bank(bg), wtile.ap[:, kc, dd * 128:(dd + 1) * 128], hT.ap[:, kc, tsl],
                             start=(kc == 0), stop=(kc == KC - 1), R=[wtile, hT], W=[PS(bg)])
                    sg = S.alloc([128, 512], F32, "sg")
                    S.act(sg.ap, S.bank(bg), AF.Sigmoid, R=[PS(bg)], W=[sg])
                    asl = acc.ap[:, dd, tsl]
                    ares = acc.sub(dd * NTOK + th * 512, dd * NTOK + (th + 1) * 512)
                    if n == 0:
                        S.tt("dve", asl, S.bank(bu), sg.ap, ALU.mult, R=[PS(bu), sg], W=[ares])
                    else:
                        S.tt("dve", sg.ap, S.bank(bu), sg.ap, ALU.mult, R=[PS(bu), sg], W=[sg])
                        if n < 3:
                            S.tt("dve", asl, asl, sg.ap, ALU.add, R=[ares, sg], W=[ares])
                        else:
                            S.tt("dve", mT.ap[:, dc, tsl], asl, sg.ap, ALU.add, R=[ares, sg], W=[mT])
                    S.release(sg)
            if n == 3:
                S.release(acc)
        S.release(wub, zs, hT)
        ring_free()
        if l == 0:
            while mod1:
                try:
                    next(mod1[0])
                except StopIteration:
                    mod1.pop()

        S.phase = 'L%d.P5' % l
        wob = S.alloc([128, KC, D], BF16, "wob")
        S.dma(wob.ap, w_o[l].rearrange("(c p) d -> p c d", p=128), writes=[wob], eng="pool")
        Gt = S.alloc([128, D], F32, "Gt")
        bcast_tile(Gt, modT[l].ap[:, 16:24], [modT[l]])
        if l + 1 < DEPTH:
            hTn, At_, Bt_ = p1_setup(l + 1)
            S.phase = 'L%d.P5' % l
            nxt_gen = lambda b_: p1_block(b_, hTn, At_, Bt_, ss8)
        else:
            gf = S.alloc([128, D], F32, "gf")
            S.dma(gf.ap, g_fin_b, writes=[gf])
            nxt_gen = lambda b_: final_block(b_, gf, ss8)
        ss8 = S.alloc([128, NB], F32, "ss8")
        junk = S.alloc([128, D], BF16, "junk")
        for b in range(NB):
            for hh in range(2):
                bk = S.nbank()
                for dc in range(KC):
                    S.mm(S.bank(bk), mT.ap[:, dc, b * 128:(b + 1) * 128], wob.ap[:, dc, hh * 512:(hh + 1) * 512],
                         start=(dc == 0), stop=(dc == KC - 1), R=[mT, wob], W=[PS(bk)])
                t = S.alloc([128, 512], F32, "ot")
                S.tt("dve", t.ap, S.bank(bk), Gt.ap[:, hh * 512:(hh + 1) * 512], ALU.mult, R=[PS(bk), Gt], W=[t])
                xsl = Xh[0].ap[:, b, hh * 512:(hh + 1) * 512]
                xr = Xh[0].sub(b * D + hh * 512, b * D + (hh + 1) * 512)
                S.tt("dve", xsl, xsl, t.ap, ALU.add, R=[xr, t], W=[xr])
                S.release(t)
            if l + 1 < DEPTH:
                xs_tok.append(S.dma(xs[b * 128:(b + 1) * 128, :], Xh[0].ap[:, b, :], reads=[Xh[0].sub(b * D, (b + 1) * D)]))
            sq_block(b, ss8, junk)
        rstd_finish(ss8)
        for _ in interleave([nxt_gen(b) for b in range(NB)], 3):
            pass
        S.release(ss8, junk)
        S.release(wob, Gt, mT, Xh[0])
        if l + 1 < DEPTH:
            S.release(At_, Bt_)
            hT_next[0] = hTn
        else:
            S.release(gf)

    out_tokens.extend(S.extra_out)
    S.finish_wait(out_tokens)
    print("[kernel] ops=%d peak_words=%d starve=%s" % (S.nops, S.peak, STARVE))
    S.emit()
    build_program.marks = S.marks
    return nc


def kernel(**inp):
    f = lambda a: np.ascontiguousarray(np.asarray(a, dtype=np.float32))
    L = DEPTH
    x_prompt, x_sample = f(inp["x_prompt"]), f(inp["x_sample"])
    cache_k, cache_v, state = f(inp["cache_k"]), f(inp["cache_v"]), f(inp["state_rwkv"])
    c_s, c_ctx = f(inp["c"]), f(inp["c_ctx"])
    shared = {}
    shared.update(host_shared_consts())
    shared["w_mod"] = f(inp["w_mod"])
    shared["b_modT"] = f(f(inp["b_mod"]).reshape(L, 24, 128).transpose(0, 2, 1))
    shared["g_normT"] = f(f(inp["g_norm"]).reshape(L, 8, 128).transpose(0, 2, 1))
    shared["g_fin_b"] = f(np.broadcast_to(f(inp["g_final"])[None, :], (128, D)))
    shared["w_in"] = f(inp["w_in"])
    shared["w_sT"] = f(f(inp["w_s"]).transpose(0, 3, 1, 2).reshape(L, 128, 512))
    bs = f(inp["b_s"]).reshape(L, 2, 2, 128).transpose(0, 2, 1, 3)
    shared["b_sT"] = f(np.broadcast_to(bs[:, :, None, :, :], (L, 2, 64, 2, 128)).reshape(L, 128, 256))
    pw = f(inp["pool_w"])
    bd = np.zeros((L, 256, 256), np.float32)
    for g in range(4):
        bd[:, g * 64:(g + 1) * 64, g * 64:(g + 1) * 64] = pw[:, g]
    shared["poolw"] = f(bd.reshape(L, 2, 128, 256).transpose(0, 2, 1, 3).reshape(L, 128, 512))
    shared["pscaleT"] = f(f(inp["pool_scale"]).reshape(L, 2, 128).transpose(0, 2, 1))
    sk = f(inp["att_sink"])
    shared["sinkb"] = f(np.broadcast_to(np.concatenate([-sk, sk], axis=1)[:, None, :], (L, 128, 8)))
    shared["w_up"] = f(inp["w_up"])
    shared["w_o"] = f(inp["w_o"])
    vecs = np.stack([f(inp["rw_k_k"]), f(inp["rw_k_a"]), f(inp["rw_r_k"]).reshape(L, 256),
                     f(inp["rw_ln_g"]), f(inp["rw_ln_b"])], axis=1)
    shared["rwvec"] = f(np.broadcast_to(vecs[:, None, :, :], (L, 128, 5, 256)).reshape(L, 128, 1280))
    w0, a0 = f(inp["rw_w0"]), f(inp["rw_a0"])
    shared["rwrow"] = f(np.concatenate([w0[:, 0], w0[:, 1], a0[:, 0], a0[:, 1]], axis=1))
    wu, au = f(inp["rw_w_up"]), f(inp["rw_a_up"])
    ru = np.zeros((L, 128, 4, 256), np.float32)
    ru[:, 0:64, 0] = wu[:, 0]
    ru[:, 0:64, 1] = wu[:, 1]
    ru[:, 64:128, 2] = au[:, 0]
    ru[:, 64:128, 3] = au[:, 1]
    shared["rwup"] = f(ru.reshape(L, 128, 1024))
    cst = [host_consts(False), host_consts(True)]
    in_maps = []
    for c in range(8):
        m = dict(shared)
        samp = c >= 4
        m.update(cst[1 if samp else 0])
        if samp:
            b = c - 4
            m["xin"] = f(x_sample[b])
            cond = c_s[b]
            m["ckT"] = f(cache_k[b].transpose(0, 2, 3, 1))
            cv = cache_v[b]
            m["cvd"] = f(np.broadcast_to(cv[:, :, :, None, :], (L, 512, 2, 2, 64)).reshape(L, 512, 256))
            si = np.zeros((L, 2, 4, 64, 256), np.float32)
            st = state[b]
            stT = st.transpose(0, 1, 4, 2, 3).reshape(L, 2, 64, 256)
            si[:, 0, 0] = stT[:, 0]
            si[:, 1, 3] = stT[:, 1]
            m["sinit"] = si
            m["keepc"] = np.ones((64, 1), np.float32)
        else:
            m["xin"] = f(x_prompt[4 * c:4 * c + 4].reshape(NTOK, D))
            cond = c_ctx
            m["ckT"] = np.zeros((L, 2, 64, 512), np.float32)
            m["cvd"] = np.zeros((L, 512, 256), np.float32)
            m["sinit"] = np.zeros((L, 2, 4, 64, 256), np.float32)
            m["keepc"] = np.zeros((64, 1), np.float32)
        m["condT"] = f(cond.reshape(8, 128).T)
        in_maps.append(m)
    nc = build_program()
    res = run_bass_kernel_spmd(nc, in_maps, core_ids=list(range(8)))
    R = res.results
    y_prompt = np.concatenate([R[c]["y_out"].reshape(4, 256, D) for c in range(4)], axis=0)
    y_sample = np.stack([R[c]["y_out"] for c in range(4, 8)], axis=0)
    nk = np.concatenate([R[c]["k_out"].reshape(L, 4, 256, 2, 64).transpose(1, 0, 2, 3, 4) for c in range(4)], axis=0)
    nv = np.concatenate([R[c]["v_out"].reshape(L, 4, 256, 2, 64).transpose(1, 0, 2, 3, 4) for c in range(4)], axis=0)
    ns = np.concatenate([R[c]["s_out"].reshape(L, 2, 4, 64, 4, 64).transpose(2, 0, 1, 4, 5, 3) for c in range(4)], axis=0)
    return (f(y_prompt), f(y_sample), f(nk), f(nv), f(ns))
```
